# Optimizing a Trainium2 kernel written in Bass

```python
import math
import jax, jax.numpy as jnp
from jax import lax
import numpy as np

D_MODEL = 1024
BATCH = 16
SEQ = 4096
DEPTH = 4

N_MIXERS = 2
N_RET_LAYERS = (DEPTH + 1) // 2
N_FNO_LAYERS = DEPTH // 2
RET_HEADS = 4
RET_DK = 256
RET_DV = 512
RET_QK = RET_HEADS * RET_DK
RET_V = RET_HEADS * RET_DV
RET_IN = 2 * RET_QK + 2 * RET_V
RET_CHUNK = 128
ROPE_BASE = 10000.0
FNO_GROUPS = 4
FNO_GROUP_DIM = D_MODEL // FNO_GROUPS
D_FF = ((8 * D_MODEL // 3 + 255) // 256) * 256
PLE_DIM = 256
NORM_EPS = 1e-6

kernel_name = "bidir_retention_fnet_hybrid"


def _rmsnorm(x, gain):
    xf = x.astype(jnp.float32)
    xf = xf * lax.rsqrt(jnp.mean(xf * xf, axis=-1, keepdims=True) + NORM_EPS)
    return (xf * gain.astype(jnp.float32)).astype(x.dtype)


def _rope(x, positions):
    half = x.shape[-1] // 2
    freq = 1.0 / (ROPE_BASE ** jnp.linspace(0.0, 1.0, half, dtype=jnp.float32))
    ang = positions.astype(jnp.float32)[..., None] * freq
    cos = jnp.cos(ang)[:, :, None, :].astype(x.dtype)
    sin = jnp.sin(ang)[:, :, None, :].astype(x.dtype)
    x1, x2 = x[..., :half], x[..., half:]
    return jnp.concatenate([x1 * cos - x2 * sin, x1 * sin + x2 * cos], axis=-1)


def _chunk_retention(q, k, v, log_gamma, strict):
    B, H, S, dk = q.shape
    dv = v.shape[-1]
    C = RET_CHUNK
    N = S // C
    qc = q.astype(jnp.float32).reshape(B, H, N, C, dk)
    kc = k.astype(jnp.float32).reshape(B, H, N, C, dk)
    vc = v.astype(jnp.float32).reshape(B, H, N, C, dv)
    lg = log_gamma[:, None]
    idx = jnp.arange(C, dtype=jnp.float32)
    diff = idx[:, None] - idx[None, :]
    mask = (diff > 0) if strict else (diff >= 0)
    decay = jnp.where(mask[None], jnp.exp(lg[:, :, None] * jnp.maximum(diff, 0.0)[None]), 0.0)
    scores = jnp.einsum('bhnid,bhnjd->bhnij', qc, kc) * decay[None, :, None]
    inner = jnp.einsum('bhnij,bhnje->bhnie', scores, vc)
    xi = jnp.exp(lg * (idx + 1.0))
    zeta = jnp.exp(lg * (C - 1.0 - idx))
    g_chunk = jnp.exp(lg * float(C))

    def step(R, xs):
        q_n, k_n, v_n = xs
        cross = jnp.einsum('bhcd,bhde->bhce', q_n, R) * xi[None, :, :, None]
        R_new = g_chunk[None, :, :, None] * R + jnp.einsum('bhcd,bhce->bhde', k_n, v_n * zeta[None, :, :, None])
        return R_new, cross

    R0 = jnp.zeros((B, H, dk, dv), jnp.float32)
    xs = (jnp.moveaxis(qc, 2, 0), jnp.moveaxis(kc, 2, 0), jnp.moveaxis(vc, 2, 0))
    _, cross = lax.scan(step, R0, xs)
    out = inner + jnp.moveaxis(cross, 0, 2)
    return out.reshape(B, H, S, dv)


def _retention_mixer(h, positions, w_in, w_out, gn_gain, decay_logit):
    B, S, _ = h.shape
    proj = h @ w_in
    q, k, v, g = jnp.split(proj, [RET_QK, 2 * RET_QK, 2 * RET_QK + RET_V], axis=-1)
    q = _rope(q.reshape(B, S, RET_HEADS, RET_DK), positions)
    k = _rope(k.reshape(B, S, RET_HEADS, RET_DK), positions) * (RET_DK ** -0.5)
    v = v.reshape(B, S, RET_HEADS, RET_DV)
    q, k, v = (t.transpose(0, 2, 1, 3) for t in (q, k, v))
    log_gamma = jax.nn.log_sigmoid(decay_logit.astype(jnp.float32))
    o_fwd = _chunk_retention(q, k, v, log_gamma[0], strict=False)
    o_bwd = jnp.flip(_chunk_retention(jnp.flip(q, 2), jnp.flip(k, 2), jnp.flip(v, 2), log_gamma[1], strict=True), axis=2)
    o = o_fwd + o_bwd
    mu = jnp.mean(o, axis=-1, keepdims=True)
    var = jnp.mean(jnp.square(o - mu), axis=-1, keepdims=True)
    o = (o - mu) * lax.rsqrt(var + NORM_EPS)
    o = o.transpose(0, 2, 1, 3).reshape(B, S, RET_V) * gn_gain.astype(jnp.float32)
    return (jax.nn.silu(g) * o.astype(h.dtype)) @ w_out


def _fourier_mixer(h, w_out):
    B, S, D = h.shape
    hg = h.astype(jnp.float32).reshape(B, S, FNO_GROUPS, FNO_GROUP_DIM)
    f = jnp.real(jnp.fft.fft2(hg, axes=(1, 3), norm='ortho'))
    return f.reshape(B, S, D).astype(h.dtype) @ w_out


def _swiglu(h, w_gate, w_up, w_down):
    return (jax.nn.silu(h @ w_gate) * (h @ w_up)) @ w_down


def setup_inputs(seed: int = 0) -> dict:
    key = jax.random.key(seed)
    ks = jax.random.split(key, 20)
    f32 = jnp.float32
    nrm = lambda k, shape, fan_in: jax.random.normal(k, shape, f32) * (fan_in ** -0.5)
    x = jax.random.normal(ks[0], (BATCH, SEQ, D_MODEL), f32)
    p = jax.random.normal(ks[1], (DEPTH, BATCH, SEQ, PLE_DIM), f32)
    positions = jnp.broadcast_to(jnp.arange(SEQ, dtype=jnp.int32)[None, :], (BATCH, SEQ))
    base_logit = (5.0 + jnp.arange(RET_HEADS, dtype=f32)) * math.log(2.0)
    ret_decay_logit = base_logit[None, None, :] + 0.1 * jax.random.normal(ks[2], (N_RET_LAYERS, 2, RET_HEADS), f32)
    return {
        "x": x,
        "p": p,
        "positions": positions,
        "norm_mix": 1.0 + 0.05 * jax.random.normal(ks[3], (DEPTH, D_MODEL), f32),
        "ret_w_in": nrm(ks[4], (N_RET_LAYERS, D_MODEL, RET_IN), D_MODEL),
        "ret_w_out": nrm(ks[5], (N_RET_LAYERS, RET_V, D_MODEL), RET_V),
        "ret_gn_gain": 1.0 + 0.05 * jax.random.normal(ks[6], (N_RET_LAYERS, RET_V), f32),
        "ret_decay_logit": ret_decay_logit,
        "fno_w_out": nrm(ks[7], (N_FNO_LAYERS, D_MODEL, D_MODEL), D_MODEL),
        "norm_ffn": 1.0 + 0.05 * jax.random.normal(ks[8], (DEPTH, D_MODEL), f32),
        "ffn_w_gate": nrm(ks[9], (DEPTH, D_MODEL, D_FF), D_MODEL),
        "ffn_w_up": nrm(ks[10], (DEPTH, D_MODEL, D_FF), D_MODEL),
        "ffn_w_down": nrm(ks[11], (DEPTH, D_FF, D_MODEL), D_FF),
        "norm_ple": 1.0 + 0.05 * jax.random.normal(ks[12], (DEPTH, D_MODEL), f32),
        "ple_w_gate": nrm(ks[13], (DEPTH, D_MODEL, D_MODEL), D_MODEL),
        "ple_w_proj": nrm(ks[14], (DEPTH, PLE_DIM, D_MODEL), PLE_DIM),
        "final_norm": 1.0 + 0.05 * jax.random.normal(ks[15], (D_MODEL,), f32),
    }


def reference(x, p, positions, norm_mix, ret_w_in, ret_w_out, ret_gn_gain, ret_decay_logit,
              fno_w_out, norm_ffn, ffn_w_gate, ffn_w_up, ffn_w_down,
              norm_ple, ple_w_gate, ple_w_proj, final_norm):
    for i in range(DEPTH):
        h = _rmsnorm(x, norm_mix[i])
        j = i // N_MIXERS
        if i % N_MIXERS == 0:
            x = x + _retention_mixer(h, positions, ret_w_in[j], ret_w_out[j], ret_gn_gain[j], ret_decay_logit[j])
        else:
            x = x + _fourier_mixer(h, fno_w_out[j])
        x = x + _swiglu(_rmsnorm(x, norm_ffn[i]), ffn_w_gate[i], ffn_w_up[i], ffn_w_down[i])
        gate = jax.nn.sigmoid(_rmsnorm(x, norm_ple[i]) @ ple_w_gate[i])
        x = x + gate * (p[i] @ ple_w_proj[i])
    return _rmsnorm(x, final_norm)
```

```python
import contextlib
import numpy as np
import ml_dtypes
import concourse.bass as bass
import concourse.mybir as mybir
from concourse.bass_utils import run_bass_kernel_spmd

F32 = mybir.dt.float32
BF16 = mybir.dt.bfloat16
I32 = mybir.dt.int32
AF = mybir.ActivationFunctionType
ALU = mybir.AluOpType

D = 1024
H = 4
DK = 256
DV = 512
VW = 2048
DFF = 2816
NF = DFF // 128
PLE = 256
DEPTH = 4
EPS = 1e-6
NB = 2
CS = 128 * NB


class Buf:
    __slots__ = ("w", "r", "name", "strict")

    def __init__(self, name="", strict=False):
        self.w = None
        self.r = {}
        self.name = name
        self.strict = strict


class KB:
    def __init__(self, nc, es, ndma=32):
        self.nc = nc
        self.es = es
        self.E = {"pe": nc.tensor, "act": nc.scalar, "dve": nc.vector, "pool": nc.gpsimd, "sp": nc.sync}
        self.sem = {e: es.enter_context(nc.semaphore("s_" + e)) for e in ("pe", "act", "dve", "pool")}
        self.cnt = {e: 0 for e in self.sem}
        self.seen = {e: {} for e in self.E}
        self.dsem = [es.enter_context(nc.semaphore("d%d" % i)) for i in range(ndma)]
        self.dcnt = [0] * ndma
        self.dring = {"sp": list(range(0, ndma - 8)), "pool": list(range(ndma - 8, ndma))}
        self.dpos = {"sp": 0, "pool": 0}
        self.uid = 0
        self.npe = 0
        self.phase_log = []

    def name(self, p):
        self.uid += 1
        return "%s_%d" % (p, self.uid)

    def _wait(self, e, tok, same=False):
        key, sem, val = tok
        if key == e and not same:
            return
        if self.seen[e].get(key, 0) >= val:
            return
        self.E[e].wait_ge(sem, val)
        self.seen[e][key] = val

    def _deps(self, e, reads, writes, same=False):
        raw_same = same or (e in ("act", "dve", "pool"))
        for b in reads:
            if b.w is not None:
                self._wait(e, b.w, raw_same)
        for b in writes:
            if b.w is not None:
                self._wait(e, b.w, same or b.strict)
            for t in b.r.values():
                self._wait(e, t, same or b.strict)

    def _mark(self, tok, reads, writes):
        for b in writes:
            b.w = tok
            b.r = {}
        for b in reads:
            old = b.r.get(tok[0])
            if old is None or old[2] < tok[2]:
                b.r[tok[0]] = tok

    def op(self, e, fn, reads=(), writes=(), signal=True):
        if e == "pe":
            self.npe += 1
        self._deps(e, reads, writes)
        ins = fn(self.E[e])
        if signal:
            self.cnt[e] += 1
            ins.then_inc(self.sem[e], 1)
            tok = (e, self.sem[e], self.cnt[e])
        else:
            tok = (e, self.sem[e], self.cnt[e] + 1)
        self._mark(tok, reads, writes)
        return ins

    def dma(self, q, out, in_, reads=(), writes=()):
        self._deps(q, reads, writes, same=True)
        ring = self.dring[q]
        k = ring[self.dpos[q] % len(ring)]
        self.dpos[q] += 1
        if self.dcnt[k] > 0:
            self._wait(q, (("d", k), self.dsem[k], 16 * self.dcnt[k]))
        ins = self.E[q].dma_start(out=out, in_=in_)
        self.dcnt[k] += 1
        ins.then_inc(self.dsem[k], 16)
        tok = (("d", k), self.dsem[k], 16 * self.dcnt[k])
        self._mark(tok, reads, writes)
        return ins

    def barrier(self, engines=None):
        for e in (engines or self.E):
            for o in self.sem:
                if o != e and self.cnt[o] > 0:
                    self._wait(e, (o, self.sem[o], self.cnt[o]))
            for k in range(len(self.dsem)):
                if self.dcnt[k] > 0:
                    self._wait(e, (("d", k), self.dsem[k], 16 * self.dcnt[k]))


class Ring:
    def __init__(self, items):
        self.items = items
        self.i = 0

    def next(self):
        it = self.items[self.i % len(self.items)]
        self.i += 1
        return it


class Phase:
    def __init__(self, kb):
        self.kb = kb
        self.es = contextlib.ExitStack()

    def __enter__(self):
        self.es.__enter__()
        return self

    def __exit__(self, *a):
        self.kb.phase_log.append(self.kb.npe)
        self.kb.barrier()
        return self.es.__exit__(*a)

    def sb(self, shape, dt, name="t"):
        t = self.es.enter_context(self.kb.nc.sbuf_tensor(self.kb.name(name), list(shape), dt))
        return t, Buf(name)

    def sb_ring(self, n, shape, dt, name="r"):
        return Ring([self.sb(shape, dt, name) for _ in range(n)])

    def ps(self, shape, dt, name="ps"):
        t = self.es.enter_context(self.kb.nc.psum_tensor(self.kb.name(name), list(shape), dt))
        return t, Buf(name)

    def ps_ring(self, n, shape, dt, name="pr"):
        return Ring([self.ps(shape, dt, name) for _ in range(n)])


def load_w(kb, ph, w_ap, kdim, ndim, name):
    nk = kdim // 128
    t, b = ph.sb([128, nk, ndim], BF16, name)
    per = max(1, min(nk, (1 << 20) // (ndim * 128 * 2)))
    for c0 in range(0, nk, per):
        c1 = min(nk, c0 + per)
        src = w_ap[c0 * 128:c1 * 128, :].rearrange("(c p) n -> p c n", p=128)
        kb.dma("pool", t[:, c0:c1, :], src, writes=[b])
    return t, b


def load_bcast(kb, ph, row_ap, n, name):
    t, b = ph.sb([128, n], F32, name)
    kb.dma("sp", t[:], row_ap.to_broadcast([128, n]), writes=[b])
    return t, b


class Norm:
    def __init__(self, kb, ph, consts, nsub):
        self.kb, self.ph, self.c, self.nsub = kb, ph, consts, nsub
        self.junk = ph.sb([128, D], F32, "junk")
        self.ss = ph.sb_ring(3, [128, 8], F32, "ss")
        self.pt = ph.ps_ring(2, [128, 1024], BF16, "pt")

    def norm(self, x_t, x_b, gain_t, gain_b, hn_t, hn_b):
        st = self.stats(x_t, x_b)
        self.apply(st, x_t, x_b, gain_t, gain_b, hn_t, hn_b)

    def stats(self, x_t, x_b):
        kb = self.kb
        ss_t, ss_b = self.ss.next()
        n = self.nsub
        for j in range(n):
            kb.op("act", lambda e, j=j: e.activation(out=self.junk[0][:], in_=x_t[:, j, :], func=AF.Square,
                                                     accum_out=ss_t[:, j:j + 1]),
                  reads=[x_b], writes=[self.junk[1], ss_b])
        kb.op("pool", lambda e: e.tensor_tensor(out=ss_t[:, 0:n], in0=ss_t[:, 0:n], in1=self.c["invd"][0][:, 0:n],
                                                op=ALU.mult), reads=[ss_b, self.c["invd"][1]], writes=[ss_b])
        kb.op("pool", lambda e: e.tensor_tensor(out=ss_t[:, 0:n], in0=ss_t[:, 0:n], in1=self.c["epsc"][0][:, 0:n],
                                                op=ALU.add), reads=[ss_b, self.c["epsc"][1]], writes=[ss_b])
        kb.op("pool", lambda e: e.tensor_tensor(out=ss_t[:, 4:4 + n], in0=ss_t[:, 0:n], in1=self.c["mhalf"][0][:, 0:n],
                                                op=ALU.pow), reads=[ss_b, self.c["mhalf"][1]], writes=[ss_b])
        return ss_t, ss_b

    def apply(self, st, x_t, x_b, gain_t, gain_b, hn_t, hn_b):
        kb = self.kb
        ss_t, ss_b = st
        for j in range(self.nsub):
            kb.op("act", lambda e, j=j: e.activation(out=hn_t[:, j, :], in_=x_t[:, j, :], func=AF.Copy,
                                                     scale=ss_t[:, 4 + j:5 + j]),
                  reads=[x_b, ss_b], writes=[hn_b])

    def transpose(self, src_t, src_b, nchunk, dst_t, dst_b, alt=0, gain=None):
        kb = self.kb
        n = self.nsub
        ident_t, ident_b = self.c["ident"]
        per = 1024 // (n * 128)
        for c0 in range(0, nchunk, per):
            pt_t, pt_b = self.pt.next()
            cn = min(per, nchunk - c0)
            for ci in range(cn):
                for j in range(n):
                    last = (ci == cn - 1 and j == n - 1)
                    kb.op("pe", lambda e, ci=ci, j=j, c0=c0: e.transpose(
                        pt_t[:, (ci * n + j) * 128:(ci * n + j + 1) * 128],
                        src_t[:, j, (c0 + ci) * 128:(c0 + ci + 1) * 128], ident_t[:]),
                        reads=[src_b, ident_b], writes=[pt_b], signal=last)
            eng = "act" if (alt + c0 // per) % 2 == 0 else "dve"
            outv = dst_t[:, c0:c0 + cn, :]
            inv = pt_t[:, 0:cn * n * 128].rearrange("p (c t) -> p c t", c=cn)
            if gain is not None:
                for ci in range(cn):
                    c = c0 + ci
                    o1 = dst_t[:, c, :]
                    i1 = pt_t[:, ci * n * 128:(ci + 1) * n * 128]
                    if (alt + c) % 2 == 0:
                        kb.op("act", lambda e, o1=o1, i1=i1, c=c: e.activation(out=o1, in_=i1, func=AF.Copy,
                                                                               scale=gain[0][:, c:c + 1]),
                              reads=[pt_b, gain[1]], writes=[dst_b])
                    else:
                        kb.op("dve", lambda e, o1=o1, i1=i1, c=c: e.tensor_scalar(out=o1, in0=i1,
                                                                                  scalar1=gain[0][:, c:c + 1],
                                                                                  scalar2=None, op0=ALU.mult),
                              reads=[pt_b, gain[1]], writes=[dst_b])
            elif eng == "act":
                kb.op("act", lambda e: e.copy(out=outv, in_=inv), reads=[pt_b], writes=[dst_b])
            else:
                kb.op("dve", lambda e: e.tensor_copy(out=outv, in_=inv), reads=[pt_b], writes=[dst_b])


def build(S, NSEQ, layers, dbg=False):
    nc = bass.Bass("TRN2", target_bir_lowering=False)
    NT = NSEQ * S
    NSC = S // CS
    ST = S // 128

    def din(name, shape, dt=F32):
        return nc.dram_tensor(name, list(shape), dt, kind="ExternalInput").ap()

    def dscr(name, shape, dt):
        return nc.dram_tensor(name, list(shape), dt, kind="ExternalOutput" if dbg else "Internal").ap()

    x_in = din("x", [NSEQ, S, D])
    p_in = din("p", [DEPTH, NSEQ, S, PLE])
    pos_in = din("pos", [NSEQ, S], I32)
    gains = din("gains", [3 * DEPTH + 1, D])
    gfm = din("gfm", [3 * DEPTH + 1, 128, 8])
    ret_w_in = din("ret_w_in", [2, D, 6144])
    ret_w_out = din("ret_w_out", [2, VW, D])
    ret_gn = din("ret_gn", [2, VW])
    ret_dl = din("ret_dl", [1, 16])
    fno_w = din("fno_w", [2, D, D])
    w_gate = din("w_gate", [DEPTH, D, DFF])
    w_up = din("w_up", [DEPTH, D, DFF])
    w_down = din("w_down", [DEPTH, DFF, D])
    w_pg = din("w_pg", [DEPTH, D, D])
    w_pp = din("w_pp", [DEPTH, PLE, D])
    freq_in = din("freq", [128, 1])
    cd_in = din("cd", [256, 512])
    _HT2 = (S // 128) // 2
    _SG = min(8, _HT2)
    dft_in = din("dft", [S // 512, 2, _HT2 // _SG, 128, _SG * 512], BF16)
    alt_in = din("alt", [1, S], BF16)
    out = nc.dram_tensor("out", [NSEQ, S, D], F32, kind="ExternalOutput").ap()

    xres = dscr("xres", [NSEQ, S, D], F32)
    cs_d = dscr("cs_d", [NSEQ, 2, 128, S], F32)
    qk_d = dscr("qk_d", [NSEQ, H, 4, 256, S], BF16)
    kt_d = dscr("kt_d", [NSEQ, 2, S, D], BF16)
    ht_d = dscr("ht_d", [D, NT], BF16)
    v_d = dscr("v_d", [NSEQ, S, VW], BF16)
    g_d = dscr("g_d", [NSEQ, S, VW], BF16)
    rb_d = dscr("rb_d", [NSEQ, H, NSC, 256, DV], BF16)
    y_d = dscr("y_d", [NSEQ, S, VW], BF16)
    ab_d = dscr("ab_d", [NSEQ, 2, S, D], BF16)
    ft_d = dscr("ft_d", [NSEQ, D, S], BF16)
    rev_d = dscr("rev_d", [NSEQ, 2, S // 2 + 1, D], BF16)

    with contextlib.ExitStack() as es:
        kb = KB(nc, es)
        g0 = Phase(kb)
        g0.__enter__()
        C = {}
        identf = g0.sb([128, 128], F32, "identf")
        C["ident"] = g0.sb([128, 128], BF16, "ident")
        C["mhalf"] = g0.sb([128, 8], F32, "mhalf")
        C["lg"] = g0.sb([128, 16], F32, "lg")
        C["freq"] = g0.sb([128, 1], F32, "freq")
        kb.op("pool", lambda e: e.memset(identf[0][:], 1.0), writes=[identf[1]])
        kb.op("pool", lambda e: e.affine_select(out=identf[0][:], in_=identf[0][:], pattern=[[-1, 128]],
                                                compare_op=ALU.is_equal, fill=0.0, base=0, channel_multiplier=1),
              reads=[identf[1]], writes=[identf[1]])
        kb.op("pool", lambda e: e.tensor_copy(out=C["ident"][0][:], in_=identf[0][:]), reads=[identf[1]],
              writes=[C["ident"][1]])
        kb.op("pool", lambda e: e.memset(C["mhalf"][0][:], -0.5), writes=[C["mhalf"][1]])
        antif = g0.sb([128, 128], F32, "antif")
        C["anti"] = g0.sb([128, 128], BF16, "anti")
        kb.op("pool", lambda e: e.memset(antif[0][:], 1.0), writes=[antif[1]])
        kb.op("pool", lambda e: e.affine_select(out=antif[0][:], in_=antif[0][:], pattern=[[1, 128]],
                                                compare_op=ALU.is_equal, fill=0.0, base=-127, channel_multiplier=1),
              reads=[antif[1]], writes=[antif[1]])
        kb.op("pool", lambda e: e.tensor_copy(out=C["anti"][0][:], in_=antif[0][:]), reads=[antif[1]],
              writes=[C["anti"][1]])
        C["invd"] = g0.sb([128, 8], F32, "invd")
        C["epsc"] = g0.sb([128, 8], F32, "epsc")
        kb.op("pool", lambda e: e.memset(C["invd"][0][:], 1.0 / D), writes=[C["invd"][1]])
        kb.op("pool", lambda e: e.memset(C["epsc"][0][:], EPS), writes=[C["epsc"][1]])
        kb.dma("sp", C["lg"][0][:], ret_dl.to_broadcast([128, 16]), writes=[C["lg"][1]])
        kb.dma("sp", C["freq"][0][:], freq_in, writes=[C["freq"][1]])
        kb.op("act", lambda e: e.activation(out=C["lg"][0][:], in_=C["lg"][0][:], func=AF.Exp, scale=-1.0),
              reads=[C["lg"][1]], writes=[C["lg"][1]])
        kb.op("act", lambda e: e.activation(out=C["lg"][0][:], in_=C["lg"][0][:], func=AF.Ln, bias=1.0),
              reads=[C["lg"][1]], writes=[C["lg"][1]])
        kb.op("dve", lambda e: e.tensor_scalar(out=C["lg"][0][:], in0=C["lg"][0][:], scalar1=-1.0, scalar2=None,
                                               op0=ALU.mult), reads=[C["lg"][1]], writes=[C["lg"][1]])

        has_ret = any(l % 2 == 0 for l in layers)
        if has_ret:
            with Phase(kb) as ph:
                W = min(S, 1024)
                pi_r = ph.sb_ring(2, [128, W], I32, "pi")
                t1_r = ph.sb_ring(2, [128, W], F32, "t1")
                ti_r = ph.sb_ring(2, [128, W], I32, "ti")
                t2_r = ph.sb_ring(2, [128, W], F32, "t2")
                t3_r = ph.sb_ring(2, [128, W], F32, "t3")
                o_r = ph.sb_ring(4, [128, W], F32, "o")
                for sq in range(NSEQ):
                    for w0 in range(0, S, W):
                        pi_t, pi_b = pi_r.next()
                        t1_t, t1_b = t1_r.next()
                        kb.dma("sp", pi_t[:], pos_in[sq:sq + 1, w0:w0 + W].to_broadcast([128, W]), writes=[pi_b])
                        kb.op("dve", lambda e: e.tensor_copy(out=t1_t[:], in_=pi_t[:]), reads=[pi_b], writes=[t1_b])
                        kb.op("dve", lambda e: e.tensor_scalar(out=t1_t[:], in0=t1_t[:], scalar1=C["freq"][0][:, 0:1],
                                                               scalar2=None, op0=ALU.mult),
                              reads=[t1_b, C["freq"][1]], writes=[t1_b])
                        for which in (1, 0):
                            ti_t, ti_b = ti_r.next()
                            t2_t, t2_b = t2_r.next()
                            t3_t, t3_b = t3_r.next()
                            o_t, o_b = o_r.next()
                            if which == 0:
                                kb.op("dve", lambda e: e.tensor_scalar(out=t1_t[:], in0=t1_t[:], scalar1=0.25,
                                                                       scalar2=None, op0=ALU.add),
                                      reads=[t1_b], writes=[t1_b])
                            kb.op("dve", lambda e: e.tensor_copy(out=ti_t[:], in_=t1_t[:]), reads=[t1_b], writes=[ti_b])
                            kb.op("dve", lambda e: e.tensor_copy(out=t2_t[:], in_=ti_t[:]), reads=[ti_b], writes=[t2_b])
                            kb.op("dve", lambda e: e.tensor_tensor(out=t3_t[:], in0=t1_t[:], in1=t2_t[:],
                                                                   op=ALU.subtract), reads=[t1_b, t2_b], writes=[t3_b])
                            kb.op("dve", lambda e: e.tensor_scalar(out=t2_t[:], in0=t3_t[:], scalar1=0.5, scalar2=None,
                                                                   op0=ALU.is_gt), reads=[t3_b], writes=[t2_b])
                            kb.op("dve", lambda e: e.tensor_tensor(out=t3_t[:], in0=t3_t[:], in1=t2_t[:],
                                                                   op=ALU.subtract), reads=[t3_b, t2_b], writes=[t3_b])
                            kb.op("dve", lambda e: e.tensor_scalar(out=t2_t[:], in0=t3_t[:], scalar1=-0.5, scalar2=None,
                                                                   op0=ALU.is_lt), reads=[t3_b], writes=[t2_b])
                            kb.op("dve", lambda e: e.tensor_tensor(out=t3_t[:], in0=t3_t[:], in1=t2_t[:], op=ALU.add),
                                  reads=[t3_b, t2_b], writes=[t3_b])
                            kb.op("act", lambda e: e.activation(out=o_t[:], in_=t3_t[:], func=AF.Sin,
                                                                scale=2.0 * np.pi), reads=[t3_b], writes=[o_b])
                            kb.dma("sp", cs_d[sq, which, :, w0:w0 + W], o_t[:], reads=[o_b])

        gains = (gains, gfm)
        for li in layers:
            lj = li // 2
            is_ret = (li % 2 == 0)
            last_layer = (li == layers[-1])
            xsrc = x_in if li == layers[0] else xres
            if is_ret:
                ret_layer(kb, C, S, NSEQ, li, lj, xsrc, cs_d, qk_d, kt_d, ht_d, v_d, g_d, rb_d, y_d,
                          gains, ret_w_in, ret_gn)
                mix_d, wmix, kmix = y_d, ret_w_out[lj], VW
            else:
                fno_layer(kb, C, S, NSEQ, li, lj, xsrc, ab_d, ft_d, gains, cd_in, dft_in, rev_d, alt_in)
                mix_d, wmix, kmix = ft_d, fno_w[lj], D
            p3a(kb, C, S, NSEQ, is_ret, mix_d, wmix, kmix, xres, xsrc)
            p3b(kb, C, S, NSEQ, li, xres, gains, w_gate[li], w_up[li], w_down[li])
            p3c(kb, C, S, NSEQ, li, xres, gains, w_pg[li], w_pp[li], p_in[li],
                out if last_layer else None)
        g0.__exit__(None, None, None)
        kb.barrier()
    nc._phase_log = kb.phase_log
    return nc


def ret_layer(kb, C, S, NSEQ, li, lj, xres, cs_d, qk_d, kt_d, ht_d, v_d, g_d, rb_d, y_d, gains, ret_w_in, ret_gn):
    NSC = S // CS
    lgc = lambda d, h: C["lg"][0][:, lj * 8 + d * 4 + h: lj * 8 + d * 4 + h + 1]
    lgb = C["lg"][1]
    with Phase(kb) as ph:
        wqk = load_w(kb, ph, ret_w_in[lj][:, 0:2048], D, 2048, "wqk")
        gain = ph.sb([128, 8], F32, "gfm")
        kb.dma("sp", gain[0][:], gains[1][li], writes=[gain[1]])
        XI = ph.sb([128, 8, 512], F32, "XI")
        ZT = ph.sb([128, 8, NB], F32, "ZT")
        it_i = ph.sb([128, 512], I32, "it_i")
        it_f = ph.sb([128, 512], F32, "it_f")
        arg = ph.sb([128, 512], F32, "arg")
        kb.op("pool", lambda e: e.iota(it_i[0][:], pattern=[[0, 512 // CS], [1, CS]], base=0, channel_multiplier=0),
              writes=[it_i[1]])
        kb.op("dve", lambda e: e.tensor_copy(out=it_f[0][:], in_=it_i[0][:]), reads=[it_i[1]], writes=[it_f[1]])
        for d in range(2):
            if d == 0:
                kb.op("dve", lambda e: e.tensor_scalar(out=arg[0][:], in0=it_f[0][:], scalar1=1.0, scalar2=None,
                                                       op0=ALU.add), reads=[it_f[1]], writes=[arg[1]])
            else:
                kb.op("dve", lambda e: e.tensor_scalar(out=arg[0][:], in0=it_f[0][:], scalar1=-1.0, scalar2=float(CS),
                                                       op0=ALU.mult, op1=ALU.add), reads=[it_f[1]], writes=[arg[1]])
            for h in range(H):
                kb.op("act", lambda e, d=d, h=h: e.activation(out=XI[0][:, d * 4 + h, :], in_=arg[0][:], func=AF.Exp,
                                                              scale=lgc(d, h)), reads=[arg[1], lgb], writes=[XI[1]])
        zi = ph.sb([128, NB], I32, "zi")
        zf = ph.sb([128, NB], F32, "zf")
        za = ph.sb([128, NB], F32, "za")
        kb.op("pool", lambda e: e.iota(zi[0][:], pattern=[[128, NB]], base=0, channel_multiplier=1), writes=[zi[1]])
        kb.op("dve", lambda e: e.tensor_copy(out=zf[0][:], in_=zi[0][:]), reads=[zi[1]], writes=[zf[1]])
        for d in range(2):
            if d == 0:
                kb.op("dve", lambda e: e.tensor_scalar(out=za[0][:], in0=zf[0][:], scalar1=-1.0, scalar2=float(CS - 1),
                                                       op0=ALU.mult, op1=ALU.add), reads=[zf[1]], writes=[za[1]])
            else:
                kb.op("dve", lambda e: e.tensor_copy(out=za[0][:], in_=zf[0][:]), reads=[zf[1]], writes=[za[1]])
            for h in range(H):
                kb.op("act", lambda e, d=d, h=h: e.activation(out=ZT[0][:, d * 4 + h, :], in_=za[0][:], func=AF.Exp,
                                                              scale=lgc(d, h)), reads=[za[1], lgb], writes=[ZT[1]])
        kb.op("dve", lambda e: e.tensor_scalar(out=ZT[0][:], in0=ZT[0][:], scalar1=1.0 / 16.0, scalar2=None,
                                               op0=ALU.mult), reads=[ZT[1]], writes=[ZT[1]])

        nm = Norm(kb, ph, C, 4)
        x_r = ph.sb_ring(2, [128, 4, D], F32, "x")
        hn_r = ph.sb_ring(1, [128, 4, D], BF16, "hn")
        hT_r = ph.sb_ring(2, [128, 8, 512], BF16, "hT")
        cs_r = ph.sb_ring(3, [128, 2, 512], F32, "cs")
        m_r = ph.sb_ring(2, [128, 4, 512], F32, "m")
        o_r = ph.sb_ring(3, [128, 2, 512], F32, "o")
        q_r = ph.sb_ring(2, [128, 3, 2, 512], BF16, "q")
        kT_r = ph.sb_ring(2, [128, 8, 512], BF16, "kT")
        kk_r = ph.sb_ring(1, [128, 2, 4, D], BF16, "kk")
        pp_r = ph.ps_ring(4, [128, 512], F32, "pp")
        pk_r = ph.ps_ring(2, [128, 1024], BF16, "pk")
        ident_t, ident_b = C["ident"]
        NTILES = NSEQ * S // 512
        pre = {}

        def do_loads(T):
            sq, s0 = divmod(T * 512, S)
            x_t, x_b = x_r.next()
            cs_t, cs_b = cs_r.next()
            kb.dma("sp", x_t[:], xres[sq, s0:s0 + 512, :].rearrange("(j p) d -> p j d", p=128), writes=[x_b])
            kb.dma("sp", cs_t[:], cs_d[sq, :, :, s0:s0 + 512].rearrange("c p s -> p c s"), writes=[cs_b])
            return x_t, x_b, cs_t, cs_b

        stA = {}

        def a_stats(T):
            x_t, x_b, cs_t, cs_b = pre.pop(T)
            stA[T] = (x_t, x_b, cs_t, cs_b, nm.stats(x_t, x_b))

        def a_norm(T):
            if T not in stA:
                a_stats(T)
            x_t, x_b, cs_t, cs_b, st = stA[T]
            hn_t, hn_b = hn_r.next()
            nm.apply(st, x_t, x_b, gain[0], gain[1], hn_t, hn_b)
            stA[T] = (cs_t, cs_b, hn_t, hn_b)

        def a_tr(T):
            cs_t, cs_b, hn_t, hn_b = stA[T]
            hT_t, hT_b = hT_r.next()
            nm.transpose(hn_t, hn_b, 8, hT_t, hT_b, gain=gain)
            kb.dma("sp", ht_d[:, T * 512:(T + 1) * 512].rearrange("(c p) t -> p c t", p=128), hT_t[:], reads=[hT_b])
            stA[T] = (cs_t, cs_b, hT_t, hT_b)

        pre[0] = do_loads(0)
        if NTILES > 1:
            pre[1] = do_loads(1)
        a_norm(0)
        a_tr(0)
        for T in range(NTILES):
            sq, s0 = divmod(T * 512, S)
            cs_t, cs_b, hT_t, hT_b = stA.pop(T)
            kT_t, kT_b = kT_r.next()
            for qk in range(2):
                for h in range(H):
                    gi = qk * 4 + h
                    if gi == 1 and T + 1 < NTILES:
                        a_stats(T + 1)
                    if gi == 3 and T + 1 < NTILES:
                        a_norm(T + 1)
                        if T + 2 < NTILES:
                            pre[T + 2] = do_loads(T + 2)
                    if gi == 6 and T + 1 < NTILES:
                        a_tr(T + 1)
                    ps = []
                    for half in range(2):
                        fc = qk * 8 + h * 2 + half
                        p_t, p_b = pp_r.next()
                        for c in range(8):
                            kb.op("pe", lambda e, c=c, fc=fc, p_t=p_t: e.matmul(
                                p_t[:], lhsT=wqk[0][:, c, fc * 128:(fc + 1) * 128], rhs=hT_t[:, c, :],
                                start=(c == 0), stop=(c == 7)), reads=[wqk[1], hT_b], writes=[p_b], signal=(c == 7))
                        ps.append((p_t, p_b))
                    m_t, m_b = m_r.next()
                    o_t, o_b = o_r.next()
                    (x1, x1b), (x2, x2b) = ps
                    cos, sin = cs_t[:, 0, :], cs_t[:, 1, :]
                    kb.op("dve", lambda e: e.tensor_tensor(out=m_t[:, 0, :], in0=x1[:], in1=cos, op=ALU.mult),
                          reads=[x1b, cs_b], writes=[m_b])
                    kb.op("dve", lambda e: e.tensor_tensor(out=m_t[:, 1, :], in0=x2[:], in1=sin, op=ALU.mult),
                          reads=[x2b, cs_b], writes=[m_b])
                    kb.op("dve", lambda e: e.tensor_tensor(out=m_t[:, 2, :], in0=x1[:], in1=sin, op=ALU.mult),
                          reads=[x1b, cs_b], writes=[m_b])
                    kb.op("dve", lambda e: e.tensor_tensor(out=m_t[:, 3, :], in0=x2[:], in1=cos, op=ALU.mult),
                          reads=[x2b, cs_b], writes=[m_b])
                    if qk == 0:
                        q_t, q_b = q_r.next()
                        kb.op("dve", lambda e: e.tensor_tensor(out=o_t[:, 0, :], in0=m_t[:, 0, :], in1=m_t[:, 1, :],
                                                               op=ALU.subtract), reads=[m_b], writes=[o_b])
                        kb.op("dve", lambda e: e.tensor_tensor(out=o_t[:, 1, :], in0=m_t[:, 2, :], in1=m_t[:, 3, :],
                                                               op=ALU.add), reads=[m_b], writes=[o_b])
                        kb.op("act", lambda e: e.copy(out=q_t[:, 0, :, :], in_=o_t[:]), reads=[o_b], writes=[q_b])
                        for d in range(2):
                            for half in range(2):
                                kb.op("pool", lambda e, d=d, half=half: e.tensor_tensor(
                                    out=q_t[:, 1 + d, half, :], in0=o_t[:, half, :], in1=XI[0][:, d * 4 + h, :],
                                    op=ALU.mult), reads=[o_b, XI[1]], writes=[q_b])
                        kb.dma("sp", qk_d[sq, h, 0:3, :, s0:s0 + 512].rearrange("k (f p) s -> p k f s", p=128),
                               q_t[:], reads=[q_b])
                    else:
                        kb.op("pool", lambda e: e.tensor_tensor(out=kT_t[:, 2 * h, :], in0=m_t[:, 0, :],
                                                                in1=m_t[:, 1, :], op=ALU.subtract),
                              reads=[m_b], writes=[kT_b])
                        kb.op("pool", lambda e: e.tensor_tensor(out=kT_t[:, 2 * h + 1, :], in0=m_t[:, 2, :],
                                                                in1=m_t[:, 3, :], op=ALU.add),
                              reads=[m_b], writes=[kT_b])
            for h in range(H):
                kb.dma("sp", qk_d[sq, h, 3, :, s0:s0 + 512].rearrange("(f p) s -> p f s", p=128),
                       kT_t[:, 2 * h:2 * h + 2, :], reads=[kT_b])
            kk_t, kk_b = kk_r.next()
            for j in range(4):
                pk_t, pk_b = pk_r.next()
                for c in range(8):
                    kb.op("pe", lambda e, c=c, j=j: e.transpose(pk_t[:, c * 128:(c + 1) * 128],
                                                                kT_t[:, c, j * 128:(j + 1) * 128], ident_t[:]),
                          reads=[kT_b, ident_b], writes=[pk_b], signal=(c == 7))
                blk = ((s0 // 128) + j) % NB
                for d in range(2):
                    for h in range(H):
                        kb.op("act", lambda e, d=d, h=h, j=j: e.activation(
                            out=kk_t[:, d, j, h * 256:(h + 1) * 256], in_=pk_t[:, h * 256:(h + 1) * 256],
                            func=AF.Copy, scale=ZT[0][:, d * 4 + h, blk:blk + 1]),
                            reads=[pk_b, ZT[1]], writes=[kk_b])
            for d in range(2):
                kb.dma("sp", kt_d[sq, d, s0:s0 + 512, :].rearrange("(j p) f -> p j f", p=128), kk_t[:, d, :, :],
                       reads=[kk_b])

    with Phase(kb) as ph:
        wvg = load_w(kb, ph, ret_w_in[lj][:, 2048:6144], D, 4096, "wvg")
        gng = load_bcast(kb, ph, ret_gn[lj:lj + 1, :], VW, "gng")
        hT_r = ph.sb_ring(2, [128, 8, 512], BF16, "hT")
        v_r = ph.sb_ring(2, [128, 4, VW], BF16, "v")
        g_r = ph.sb_ring(2, [128, 4, VW], BF16, "g")
        sg_r = ph.sb_ring(3, [128, 512], F32, "sg")
        pp_r = ph.ps_ring(6, [128, 512], F32, "pp")
        NTILES = NSEQ * S // 512
        pre = {}

        def do_loads(T):
            hT_t, hT_b = hT_r.next()
            kb.dma("sp", hT_t[:], ht_d[:, T * 512:(T + 1) * 512].rearrange("(c p) t -> p c t", p=128), writes=[hT_b])
            return hT_t, hT_b

        for T in range(NTILES):
            sq, s0 = divmod(T * 512, S)
            if T == 0:
                pre[0] = do_loads(0)
            if T + 1 < NTILES:
                pre[T + 1] = do_loads(T + 1)
            hT_t, hT_b = pre.pop(T)
            v_t, v_b = v_r.next()
            g_t, g_b = g_r.next()
            for j in range(4):
                for n in range(8):
                    p_t, p_b = pp_r.next()
                    for c in range(8):
                        kb.op("pe", lambda e, c=c, n=n, j=j, p_t=p_t: e.matmul(
                            p_t[:], lhsT=hT_t[:, c, j * 128:(j + 1) * 128], rhs=wvg[0][:, c, n * 512:(n + 1) * 512],
                            start=(c == 0), stop=(c == 7)), reads=[wvg[1], hT_b], writes=[p_b], signal=(c == 7))
                    if n < 4:
                        if n % 2 == 0:
                            kb.op("act", lambda e, n=n, j=j, p_t=p_t: e.copy(out=v_t[:, j, n * 512:(n + 1) * 512],
                                                                             in_=p_t[:]), reads=[p_b], writes=[v_b])
                        else:
                            kb.op("dve", lambda e, n=n, j=j, p_t=p_t: e.tensor_copy(
                                out=v_t[:, j, n * 512:(n + 1) * 512], in_=p_t[:]), reads=[p_b], writes=[v_b])
                    else:
                        sg_t, sg_b = sg_r.next()
                        kb.op("act", lambda e, p_t=p_t, sg_t=sg_t: e.activation(out=sg_t[:], in_=p_t[:], func=AF.Silu),
                              reads=[p_b], writes=[sg_b])
                        kb.op("pool", lambda e, n=n, j=j, sg_t=sg_t: e.tensor_tensor(
                            out=g_t[:, j, (n - 4) * 512:(n - 3) * 512], in0=sg_t[:],
                            in1=gng[0][:, (n - 4) * 512:(n - 3) * 512], op=ALU.mult),
                            reads=[sg_b, gng[1]], writes=[g_b])
            kb.dma("sp", v_d[sq, s0:s0 + 512, :].rearrange("(j p) f -> p j f", p=128), v_t[:], reads=[v_b])
            kb.dma("sp", g_d[sq, s0:s0 + 512, :].rearrange("(j p) f -> p j f", p=128), g_t[:], reads=[g_b])

    with Phase(kb) as ph:
        NO = 2 * NB - 1
        MK = ph.sb([128, H, NO, 128], F32, "MK")
        GG = ph.sb([128, 8], F32, "GG")
        csz = ph.sb([128, 1], F32, "csz")
        kb.op("pool", lambda e: e.memset(csz[0][:], float(CS)), writes=[csz[1]])
        for d in range(2):
            for h in range(H):
                kb.op("act", lambda e, d=d, h=h: e.activation(out=GG[0][:, d * 4 + h:d * 4 + h + 1], in_=csz[0][:],
                                                              func=AF.Exp, scale=lgc(d, h)),
                      reads=[csz[1], lgb], writes=[GG[1]])
        di = ph.sb([128, 128], I32, "di")
        df = ph.sb([128, 128], F32, "df")
        mpos = ph.sb([128, 128], F32, "mpos")
        mneg = ph.sb([128, 128], F32, "mneg")
        ind = ph.sb([128, 128], F32, "ind")
        inb = ph.sb([128, 128], F32, "inb")
        ef = ph.sb([128, 128], F32, "ef")
        eb = ph.sb([128, 128], F32, "eb")
        for oi in range(NO):
            off = oi - (NB - 1)
            kb.op("pool", lambda e: e.iota(di[0][:], pattern=[[1, 128]], base=-128 * off, channel_multiplier=-1),
                  reads=[di[1]], writes=[di[1]])
            kb.op("dve", lambda e: e.tensor_copy(out=df[0][:], in_=di[0][:]), reads=[di[1]], writes=[df[1]])
            kb.op("dve", lambda e: e.tensor_scalar(out=mpos[0][:], in0=df[0][:], scalar1=0.0, scalar2=None,
                                                   op0=ALU.max), reads=[df[1]], writes=[mpos[1]])
            kb.op("dve", lambda e: e.tensor_scalar(out=mneg[0][:], in0=df[0][:], scalar1=-1.0, scalar2=0.0,
                                                   op0=ALU.mult, op1=ALU.max), reads=[df[1]], writes=[mneg[1]])
            kb.op("dve", lambda e: e.tensor_scalar(out=ind[0][:], in0=df[0][:], scalar1=0.0, scalar2=1.0 / 16.0,
                                                   op0=ALU.is_ge, op1=ALU.mult), reads=[df[1]], writes=[ind[1]])
            kb.op("dve", lambda e: e.tensor_scalar(out=inb[0][:], in0=df[0][:], scalar1=0.0, scalar2=1.0 / 16.0,
                                                   op0=ALU.is_lt, op1=ALU.mult), reads=[df[1]], writes=[inb[1]])
            for h in range(H):
                kb.op("act", lambda e, h=h: e.activation(out=ef[0][:], in_=mpos[0][:], func=AF.Exp, scale=lgc(0, h)),
                      reads=[mpos[1], lgb], writes=[ef[1]])
                kb.op("act", lambda e, h=h: e.activation(out=eb[0][:], in_=mneg[0][:], func=AF.Exp, scale=lgc(1, h)),
                      reads=[mneg[1], lgb], writes=[eb[1]])
                kb.op("dve", lambda e: e.tensor_tensor(out=ef[0][:], in0=ef[0][:], in1=ind[0][:], op=ALU.mult),
                      reads=[ef[1], ind[1]], writes=[ef[1]])
                kb.op("dve", lambda e: e.tensor_tensor(out=eb[0][:], in0=eb[0][:], in1=inb[0][:], op=ALU.mult),
                      reads=[eb[1], inb[1]], writes=[eb[1]])
                kb.op("dve", lambda e, h=h, oi=oi: e.tensor_tensor(out=MK[0][:, h, oi, :], in0=ef[0][:], in1=eb[0][:],
                                                                   op=ALU.add), reads=[ef[1], eb[1]], writes=[MK[1]])

        R32 = [ph.sb([128, 2, DV], F32, "R32") for _ in range(H)]
        Rbf = [ph.sb_ring(2, [128, 2, DV], BF16, "Rbf") for _ in range(H)]
        kk_r = ph.sb_ring(6, [128, NB, 256], BF16, "kk")
        v_r = ph.sb_ring(6, [128, NB, DV], BF16, "v")
        g_r = ph.sb_ring(6, [128, NB, DV], BF16, "g")
        qk_r = ph.sb_ring(6, [128, 4, 2, CS], BF16, "qk")
        rb_r = ph.sb_ring(6, [128, 2, DV], BF16, "rb")
        sT_r = ph.sb_ring(3, [128, NB, 128], BF16, "sT")
        yn_r = ph.sb_ring(3, [128, DV], F32, "yn")
        y_r = ph.sb_ring(4, [128, NB, DV], BF16, "y")
        st_r = ph.sb_ring(6, [128, 8], F32, "st")
        junk = ph.sb([128, DV], F32, "junk2")
        ps_s = ph.ps_ring(2, [128, 512], F32, "ps_s")
        ps_o = ph.ps_ring(4, [128, 512], F32, "ps_o")
        ps_r = ph.ps_ring(2, [128, 512], F32, "ps_r")

        def state_update(h, d, kk_t, kk_b, v_t, v_b):
            r32_t, r32_b = R32[h]
            nb_t, nb_b = Rbf[h].next()
            for half in range(2):
                p_t, p_b = ps_r.next()
                for blk in range(NB):
                    kb.op("pe", lambda e, blk=blk, half=half, p_t=p_t: e.matmul(
                        p_t[:], lhsT=kk_t[:, blk, half * 128:(half + 1) * 128], rhs=v_t[:, blk, :],
                        start=(blk == 0), stop=(blk == NB - 1)), reads=[kk_b, v_b], writes=[p_b],
                        signal=(blk == NB - 1))
                kb.op("dve", lambda e, half=half, p_t=p_t: e.scalar_tensor_tensor(
                    out=r32_t[:, half, :], in0=r32_t[:, half, :], scalar=GG[0][:, d * 4 + h:d * 4 + h + 1],
                    in1=p_t[:], op0=ALU.mult, op1=ALU.add), reads=[r32_b, p_b, GG[1]], writes=[r32_b])
            kb.op("act", lambda e: e.copy(out=nb_t[:], in_=r32_t[:]), reads=[r32_b], writes=[nb_b])
            return nb_t, nb_b

        rbufs = {(sq, h, sc): Buf("rbd") for sq in range(NSEQ) for h in range(H) for sc in range(NSC)}
        for sq in range(NSEQ):
            cur = []
            for h in range(H):
                kb.op("pool", lambda e, h=h: e.memset(R32[h][0][:], 0.0), writes=[R32[h][1]])
                nb_t, nb_b = Rbf[h].next()
                kb.op("pool", lambda e, nb_t=nb_t: e.memset(nb_t[:], 0.0), writes=[nb_b])
                cur.append((nb_t, nb_b))
            def b_loads(it):
                scr, h = divmod(it, H)
                sc = NSC - 1 - scr
                s0 = sc * CS
                kk_t, kk_b = kk_r.next()
                v_t, v_b = v_r.next()
                kb.dma("sp", kk_t[:], kt_d[sq, 1, s0:s0 + CS, h * 256:(h + 1) * 256].rearrange(
                    "(b p) f -> p b f", p=128), writes=[kk_b])
                kb.dma("sp", v_t[:], v_d[sq, s0:s0 + CS, h * DV:(h + 1) * DV].rearrange("(b p) f -> p b f", p=128),
                       writes=[v_b])
                return kk_t, kk_b, v_t, v_b

            NITB = (NSC - 1) * H
            PFB = 4
            bpre = {}
            for it in range(min(PFB, NITB)):
                bpre[it] = b_loads(it)
            for scr in range(NSC):
                sc = NSC - 1 - scr
                for h in range(H):
                    kb.dma("sp", rb_d[sq, h, sc].rearrange("(f p) e -> p f e", p=128), cur[h][0][:], reads=[cur[h][1]],
                           writes=[rbufs[(sq, h, sc)]])
                    if sc == 0:
                        continue
                    it = scr * H + h
                    if it + PFB < NITB:
                        bpre[it + PFB] = b_loads(it + PFB)
                    kk_t, kk_b, v_t, v_b = bpre.pop(it)
                    cur[h] = state_update(h, 1, kk_t, kk_b, v_t, v_b)
            cur = []
            for h in range(H):
                kb.op("pool", lambda e, h=h: e.memset(R32[h][0][:], 0.0), reads=[], writes=[R32[h][1]])
                nb_t, nb_b = Rbf[h].next()
                kb.op("pool", lambda e, nb_t=nb_t: e.memset(nb_t[:], 0.0), writes=[nb_b])
                cur.append((nb_t, nb_b))
            def f_loads(it):
                sc, h = divmod(it, H)
                s0 = sc * CS
                kk_t, kk_b = kk_r.next()
                v_t, v_b = v_r.next()
                g_t, g_b = g_r.next()
                qk_t, qk_b = qk_r.next()
                rb_t, rb_b = rb_r.next()
                kb.dma("sp", qk_t[:], qk_d[sq, h, :, :, s0:s0 + CS].rearrange("k (f p) s -> p k f s", p=128),
                       writes=[qk_b])
                kb.dma("sp", kk_t[:], kt_d[sq, 0, s0:s0 + CS, h * 256:(h + 1) * 256].rearrange(
                    "(b p) f -> p b f", p=128), writes=[kk_b])
                kb.dma("sp", v_t[:], v_d[sq, s0:s0 + CS, h * DV:(h + 1) * DV].rearrange("(b p) f -> p b f", p=128),
                       writes=[v_b])
                kb.dma("sp", g_t[:], g_d[sq, s0:s0 + CS, h * DV:(h + 1) * DV].rearrange("(b p) f -> p b f", p=128),
                       writes=[g_b])
                kb.dma("sp", rb_t[:], rb_d[sq, h, sc].rearrange("(f p) e -> p f e", p=128), reads=[rbufs[(sq, h, sc)]],
                       writes=[rb_b])
                return kk_t, kk_b, v_t, v_b, g_t, g_b, qk_t, qk_b, rb_t, rb_b

            NIT = NSC * H
            PF = 3
            fpre = {}
            for it in range(min(PF, NIT)):
                fpre[it] = f_loads(it)
            backq = []
            SKEW = 3
            for sc in range(NSC):
                s0 = sc * CS
                for h in range(H):
                    it = sc * H + h
                    if it + PF < NIT:
                        fpre[it + PF] = f_loads(it + PF)
                    kk_t, kk_b, v_t, v_b, g_t, g_b, qk_t, qk_b, rb_t, rb_b = fpre.pop(it)
                    rf_t, rf_b = cur[h]
                    y_t, y_b = y_r.next()
                    for ib in range(NB):
                        isl = slice(ib * 128, (ib + 1) * 128)
                        s_t, s_b = ps_s.next()
                        sT_t, sT_b = sT_r.next()
                        for jb in range(NB):
                            for half in range(2):
                                kb.op("pe", lambda e, jb=jb, half=half: e.matmul(
                                    s_t[:, jb * 128:(jb + 1) * 128], lhsT=qk_t[:, 3, half, jb * 128:(jb + 1) * 128],
                                    rhs=qk_t[:, 0, half, isl], start=(half == 0), stop=(half == 1)),
                                    reads=[qk_b], writes=[s_b], signal=(half == 1 and jb == NB - 1))
                        for jb in range(NB):
                            kb.op("dve", lambda e, jb=jb: e.tensor_tensor(
                                out=sT_t[:, jb, :], in0=s_t[:, jb * 128:(jb + 1) * 128],
                                in1=MK[0][:, h, jb - ib + NB - 1, :], op=ALU.mult),
                                reads=[s_b, MK[1]], writes=[sT_b])
                        o_t, o_b = ps_o.next()
                        nmm = NB + 4
                        k = 0
                        for half in range(2):
                            kb.op("pe", lambda e, half=half, k=k: e.matmul(
                                o_t[:], lhsT=qk_t[:, 1, half, isl], rhs=rf_t[:, half, :], start=(k == 0), stop=False),
                                reads=[qk_b, rf_b], writes=[o_b], signal=False)
                            k += 1
                        for half in range(2):
                            kb.op("pe", lambda e, half=half: e.matmul(
                                o_t[:], lhsT=qk_t[:, 2, half, isl], rhs=rb_t[:, half, :], start=False, stop=False),
                                reads=[qk_b, rb_b], writes=[o_b], signal=False)
                            k += 1
                        for jb in range(NB):
                            kb.op("pe", lambda e, jb=jb: e.matmul(
                                o_t[:], lhsT=sT_t[:, jb, :], rhs=v_t[:, jb, :], start=False, stop=(jb == NB - 1)),
                                reads=[sT_b, v_b], writes=[o_b], signal=(jb == NB - 1))
                        def back(o_t=o_t, o_b=o_b, ib=ib, y_t=y_t, y_b=y_b, g_t=g_t, g_b=g_b, h=h, s0=s0):
                            st_t, st_b = st_r.next()
                            yn_t, yn_b = yn_r.next()
                            kb.op("act", lambda e: e.activation(out=junk[0][:], in_=o_t[:], func=AF.Identity,
                                                                accum_out=st_t[:, 0:1]),
                                  reads=[o_b], writes=[junk[1], st_b])
                            kb.op("act", lambda e: e.activation(out=junk[0][:], in_=o_t[:], func=AF.Square,
                                                                accum_out=st_t[:, 1:2]),
                                  reads=[o_b], writes=[junk[1], st_b])
                            kb.op("dve", lambda e: e.tensor_scalar(out=st_t[:, 2:4], in0=st_t[:, 0:2], scalar1=1.0 / DV,
                                                                   scalar2=None, op0=ALU.mult), reads=[st_b], writes=[st_b])
                            kb.op("dve", lambda e: e.tensor_tensor(out=st_t[:, 4:5], in0=st_t[:, 2:3], in1=st_t[:, 2:3],
                                                                   op=ALU.mult), reads=[st_b], writes=[st_b])
                            kb.op("dve", lambda e: e.scalar_tensor_tensor(out=st_t[:, 5:6], in0=st_t[:, 3:4], scalar=EPS,
                                                                          in1=st_t[:, 4:5], op0=ALU.add, op1=ALU.subtract),
                                  reads=[st_b], writes=[st_b])
                            kb.op("pool", lambda e: e.tensor_tensor(out=st_t[:, 6:7], in0=st_t[:, 5:6],
                                                                    in1=C["mhalf"][0][:, 0:1], op=ALU.pow),
                                  reads=[st_b, C["mhalf"][1]], writes=[st_b])
                            kb.op("dve", lambda e: e.scalar_tensor_tensor(out=st_t[:, 7:8], in0=st_t[:, 2:3], scalar=-1.0,
                                                                          in1=st_t[:, 6:7], op0=ALU.mult, op1=ALU.mult),
                                  reads=[st_b], writes=[st_b])
                            kb.op("act", lambda e: e.activation(out=yn_t[:], in_=o_t[:], func=AF.Identity,
                                                                scale=st_t[:, 6:7], bias=st_t[:, 7:8]),
                                  reads=[o_b, st_b], writes=[yn_b])
                            kb.op("pool", lambda e, ib=ib: e.tensor_tensor(out=y_t[:, ib, :], in0=yn_t[:], in1=g_t[:, ib, :],
                                                                           op=ALU.mult), reads=[yn_b, g_b], writes=[y_b])
                            if ib == NB - 1:
                                kb.dma("pool", y_d[sq, s0:s0 + CS, h * DV:(h + 1) * DV].rearrange("(b p) f -> p b f", p=128), y_t[:],
                                       reads=[y_b])
                        backq.append(back)
                        while len(backq) > SKEW:
                            backq.pop(0)()
                    if sc < NSC - 1:
                        cur[h] = state_update(h, 0, kk_t, kk_b, v_t, v_b)
            while backq:
                backq.pop(0)()


def fno_layer(kb, C, S, NSEQ, li, lj, xres, ab_d, ft_d, gains, cd_in, dft_in, rev_d, alt_in):
    ST = S // 128
    with Phase(kb) as ph:
        cd = load_w(kb, ph, cd_in, 256, 512, "cd")
        gain = ph.sb([128, 8], F32, "gfm")
        kb.dma("sp", gain[0][:], gains[1][li], writes=[gain[1]])
        nm = Norm(kb, ph, C, 4)
        x_r = ph.sb_ring(2, [128, 4, D], F32, "x")
        hn_r = ph.sb_ring(2, [128, 4, D], BF16, "hn")
        hT_r = ph.sb_ring(2, [128, 8, 512], BF16, "hT")
        ab_r = ph.sb_ring(2, [128, 2, 4, D], BF16, "ab")
        rv_r = ph.sb_ring(3, [128, 2, D], BF16, "rv")
        pp_r = ph.ps_ring(4, [128, 512], F32, "pp")
        zr = ph.sb([1, D], BF16, "zr")
        kb.op("pool", lambda e: e.memset(zr[0][:], 0.0), writes=[zr[1]])
        for sq_ in range(NSEQ):
            for a_ in range(2):
                kb.dma("sp", rev_d[sq_, a_, 0:1, :], zr[0][:], reads=[zr[1]])
        NTILES = NSEQ * S // 512
        pre = {}

        def do_loads(T):
            sq, s0 = divmod(T * 512, S)
            x_t, x_b = x_r.next()
            kb.dma("sp", x_t[:], xres[sq, s0:s0 + 512, :].rearrange("(j p) d -> p j d", p=128), writes=[x_b])
            return x_t, x_b

        stA = {}

        def a_stats(T):
            x_t, x_b = pre.pop(T)
            stA[T] = (x_t, x_b, nm.stats(x_t, x_b))

        def a_norm(T):
            if T not in stA:
                a_stats(T)
            x_t, x_b, st = stA[T]
            hn_t, hn_b = hn_r.next()
            nm.apply(st, x_t, x_b, None, None, hn_t, hn_b)
            stA[T] = (hn_t, hn_b)

        def a_tr(T):
            hn_t, hn_b = stA[T]
            hT_t, hT_b = hT_r.next()
            nm.transpose(hn_t, hn_b, 8, hT_t, hT_b, gain=gain)
            stA[T] = (hT_t, hT_b)

        pre[0] = do_loads(0)
        a_norm(0)
        a_tr(0)
        if NTILES > 1:
            pre[1] = do_loads(1)
        for T in range(NTILES):
            sq, s0 = divmod(T * 512, S)
            hT_t, hT_b = stA.pop(T)
            ab_t, ab_b = ab_r.next()
            for j in range(4):
                for g in range(4):
                    gi = j * 4 + g
                    if gi == 0 and T + 1 < NTILES:
                        a_stats(T + 1)
                    if gi == 5 and T + 1 < NTILES:
                        a_norm(T + 1)
                        if T + 2 < NTILES:
                            pre[T + 2] = do_loads(T + 2)
                    if gi == 11 and T + 1 < NTILES:
                        a_tr(T + 1)
                    p_t, p_b = pp_r.next()
                    for half in range(2):
                        kb.op("pe", lambda e, half=half, g=g, j=j, p_t=p_t: e.matmul(
                            p_t[:], lhsT=hT_t[:, 2 * g + half, j * 128:(j + 1) * 128], rhs=cd[0][:, half, :],
                            start=(half == 0), stop=(half == 1)), reads=[hT_b, cd[1]], writes=[p_b],
                            signal=(half == 1))
                    outv = ab_t[:, :, j, g * 256:(g + 1) * 256]
                    inv = p_t[:].rearrange("p (a l) -> p a l", a=2)
                    if (j * 4 + g) % 2 == 0:
                        kb.op("act", lambda e, outv=outv, inv=inv: e.copy(out=outv, in_=inv), reads=[p_b], writes=[ab_b])
                    else:
                        kb.op("dve", lambda e, outv=outv, inv=inv: e.tensor_copy(out=outv, in_=inv), reads=[p_b],
                              writes=[ab_b])
            for a in range(2):
                kb.dma("sp", ab_d[sq, a, s0:s0 + 512, :].rearrange("(j p) f -> p j f", p=128), ab_t[:, a, :, :],
                       reads=[ab_b])
            for j in range(4):
                t = s0 // 128 + j
                if t < ST // 2:
                    continue
                rv_t, rv_b = rv_r.next()
                for a in range(2):
                    for n in range(2):
                        p_t, p_b = pp_r.next()
                        kb.op("pe", lambda e, a=a, n=n, j=j, p_t=p_t: e.matmul(
                            p_t[:], lhsT=C["anti"][0][:], rhs=ab_t[:, a, j, n * 512:(n + 1) * 512], start=True, stop=True),
                            reads=[ab_b, C["anti"][1]], writes=[p_b])
                        if (a + n) % 2 == 0:
                            kb.op("act", lambda e, a=a, n=n, p_t=p_t: e.copy(out=rv_t[:, a, n * 512:(n + 1) * 512],
                                                                             in_=p_t[:]), reads=[p_b], writes=[rv_b])
                        else:
                            kb.op("dve", lambda e, a=a, n=n, p_t=p_t: e.tensor_copy(
                                out=rv_t[:, a, n * 512:(n + 1) * 512], in_=p_t[:]), reads=[p_b], writes=[rv_b])
                r0 = (ST - 1 - t) * 128 + 1
                for a in range(2):
                    kb.dma("sp", rev_d[sq, a, r0:r0 + 128, :], rv_t[:, a, :], reads=[rv_b])
    with Phase(kb) as ph:
        HT2 = ST // 2
        SG = min(8, HT2)
        NG = HT2 // SG
        L_r = ph.sb_ring(2, [128, 2, HT2, 512], BF16, "L")
        M_r = ph.sb_ring(2, [128, 2, HT2, 512], BF16, "M")
        mid_r = ph.sb_ring(2, [1, 512], BF16, "mid")
        alt = ph.sb([1, S], BF16, "alt")
        kb.dma("sp", alt[0][:], alt_in, writes=[alt[1]])
        pc_r = ph.sb_ring(6, [128, SG, 512], BF16, "pc")
        fo_r = ph.sb_ring(2, [128, 4, 512], BF16, "fo")
        pp_r = ph.ps_ring(8, [128, 512], F32, "pp")
        def f2_prep(sq, chh):
                L_t, L_b = L_r.next()
                M_t, M_b = M_r.next()
                mid_t, mid_b = mid_r.next()
                csl = slice(chh * 512, (chh + 1) * 512)
                for a in range(2):
                    for s1 in range(0, HT2, 8):
                        s2 = min(HT2, s1 + 8)
                        kb.dma("sp", L_t[:, a, s1:s2, :],
                               ab_d[sq, a, s1 * 128:s2 * 128, csl].rearrange("(t p) c -> p t c", p=128), writes=[L_b])
                        kb.dma("sp", M_t[:, a, s1:s2, :],
                               rev_d[sq, a, s1 * 128:s2 * 128, csl].rearrange("(t p) c -> p t c", p=128), writes=[M_b])
                kb.dma("sp", mid_t[:], rev_d[sq, 0, S // 2:S // 2 + 1, csl], writes=[mid_b])
                kb.op("dve", lambda e: e.tensor_tensor(out=L_t[:, 0, :, :], in0=L_t[:, 0, :, :], in1=M_t[:, 0, :, :],
                                                       op=ALU.add), reads=[L_b, M_b], writes=[L_b])
                kb.op("dve", lambda e: e.tensor_tensor(out=L_t[:, 1, :, :], in0=L_t[:, 1, :, :], in1=M_t[:, 1, :, :],
                                                       op=ALU.subtract), reads=[L_b, M_b], writes=[L_b])
                return L_t, L_b, mid_t, mid_b

        jobs = [(sq, chh) for sq in range(NSEQ) for chh in range(2)]
        prep = {0: f2_prep(*jobs[0])}
        for ji, (sq, chh) in enumerate(jobs):
                L_t, L_b, mid_t, mid_b = prep.pop(ji)
                for k0 in range(0, S, 512):
                    if k0 == min(512, S - 512) and ji + 1 < len(jobs):
                        prep[ji + 1] = f2_prep(*jobs[ji + 1])
                    banks = [pp_r.next() for _ in range(4)]
                    n_acc = 2 * HT2 + 1
                    for cc in range(4):
                        p_t, p_b = banks[cc]
                        kb.op("pe", lambda e, cc=cc, p_t=p_t: e.matmul(
                            p_t[:], lhsT=mid_t[0:1, cc * 128:(cc + 1) * 128], rhs=alt[0][0:1, k0:k0 + 512],
                            start=True, stop=False), reads=[mid_b, alt[1]], writes=[p_b], signal=False)
                    idx = 1
                    for g in range(NG):
                        for a in range(2):
                            pc_t, pc_b = pc_r.next()
                            kb.dma("sp", pc_t[:], dft_in[k0 // 512, a, g].rearrange("p (t k) -> p t k", t=SG),
                                   writes=[pc_b])
                            for t in range(SG):
                                for cc in range(4):
                                    p_t, p_b = banks[cc]
                                    kb.op("pe", lambda e, t=t, cc=cc, a=a, g=g, p_t=p_t, idx=idx: e.matmul(
                                        p_t[:], lhsT=L_t[:, a, g * SG + t, cc * 128:(cc + 1) * 128], rhs=pc_t[:, t, :],
                                        start=False, stop=(idx == n_acc - 1)),
                                        reads=[L_b, pc_b], writes=[p_b], signal=(idx == n_acc - 1 or t == SG - 1))
                                idx += 1
                    fo_t, fo_b = fo_r.next()
                    for cc in range(4):
                        p_t, p_b = banks[cc]
                        if cc % 2 == 0:
                            kb.op("act", lambda e, cc=cc, p_t=p_t: e.copy(out=fo_t[:, cc, :], in_=p_t[:]), reads=[p_b],
                                  writes=[fo_b])
                        else:
                            kb.op("dve", lambda e, cc=cc, p_t=p_t: e.tensor_copy(out=fo_t[:, cc, :], in_=p_t[:]),
                                  reads=[p_b], writes=[fo_b])
                    kb.dma("sp", ft_d[sq, chh * 512:(chh + 1) * 512, k0:k0 + 512].rearrange("(c p) k -> p c k", p=128),
                           fo_t[:], reads=[fo_b])


def p3a(kb, C, S, NSEQ, is_ret, mix_d, wmix_ap, kmix, xres, xsrc):
    nkc = kmix // 128
    with Phase(kb) as ph:
        wo = load_w(kb, ph, wmix_ap, kmix, D, "wo")
        nm = Norm(kb, ph, C, 4)
        x_r = ph.sb_ring(3, [128, 4, D], F32, "x")
        y_r = ph.sb_ring(3, [128, 4, kmix], BF16, "y") if is_ret else None
        mT_r = ph.sb_ring(2 if is_ret else 3, [128, nkc, 512], BF16, "mT")
        pp_r = ph.ps_ring(4, [128, 512], F32, "pp")
        NTILES = NSEQ * S // 512
        pre = {}

        def do_loads(T):
            sq, s0 = divmod(T * 512, S)
            x_t, x_b = x_r.next()
            kb.dma("sp", x_t[:], xsrc[sq, s0:s0 + 512, :].rearrange("(j p) d -> p j d", p=128), writes=[x_b])
            if is_ret:
                y_t, y_b = y_r.next()
                kb.dma("sp", y_t[:], mix_d[sq, s0:s0 + 512, :].rearrange("(j p) f -> p j f", p=128), writes=[y_b])
                return x_t, x_b, y_t, y_b
            mT_t, mT_b = mT_r.next()
            kb.dma("sp", mT_t[:], mix_d[sq, :, s0:s0 + 512].rearrange("(c p) k -> p c k", p=128), writes=[mT_b])
            return x_t, x_b, mT_t, mT_b

        stA = {}

        def a_tr(T):
            if is_ret:
                x_t, x_b, y_t, y_b = pre.pop(T)
                mT_t, mT_b = mT_r.next()
                nm.transpose(y_t, y_b, nkc, mT_t, mT_b)
                stA[T] = (x_t, x_b, mT_t, mT_b)
            else:
                stA[T] = pre.pop(T)

        pre[0] = do_loads(0)
        if NTILES > 1:
            pre[1] = do_loads(1)
        a_tr(0)
        for T in range(NTILES):
            sq, s0 = divmod(T * 512, S)
            x_t, x_b, mT_t, mT_b = stA.pop(T)
            if T + 2 < NTILES:
                pre[T + 2] = do_loads(T + 2)
            for j in range(4):
                for n in range(2):
                    if j == 2 and n == 0 and T + 1 < NTILES:
                        a_tr(T + 1)
                    p_t, p_b = pp_r.next()
                    for c in range(nkc):
                        kb.op("pe", lambda e, c=c, n=n, j=j, p_t=p_t: e.matmul(
                            p_t[:], lhsT=mT_t[:, c, j * 128:(j + 1) * 128], rhs=wo[0][:, c, n * 512:(n + 1) * 512],
                            start=(c == 0), stop=(c == nkc - 1)), reads=[mT_b, wo[1]], writes=[p_b],
                            signal=(c == nkc - 1))
                    kb.op("dve", lambda e, n=n, j=j, p_t=p_t: e.tensor_tensor(
                        out=x_t[:, j, n * 512:(n + 1) * 512], in0=x_t[:, j, n * 512:(n + 1) * 512], in1=p_t[:],
                        op=ALU.add), reads=[x_b, p_b], writes=[x_b])
            kb.dma("sp", xres[sq, s0:s0 + 512, :].rearrange("(j p) d -> p j d", p=128), x_t[:], reads=[x_b])


def p3b(kb, C, S, NSEQ, li, xres, gains, wg_ap, wu_ap, wd_ap):
    TT = 256
    NS = TT // 128
    with Phase(kb) as ph:
        wg = load_w(kb, ph, wg_ap, D, DFF, "wg")
        wu = load_w(kb, ph, wu_ap, D, DFF, "wu")
        wd = load_w(kb, ph, wd_ap, DFF, D, "wd")
        gain = ph.sb([128, 8], F32, "gfm")
        kb.dma("sp", gain[0][:], gains[1][DEPTH + li], writes=[gain[1]])
        nm = Norm(kb, ph, C, NS)
        x_r = ph.sb_ring(3, [128, NS, D], F32, "x")
        hn_r = ph.sb_ring(2, [128, NS, D], BF16, "hn")
        hT_r = ph.sb_ring(2, [128, 8, TT], BF16, "hT")
        aT_r = ph.sb_ring(1, [128, NF, TT], BF16, "aT")
        sg_r = ph.sb_ring(3, [128, TT], F32, "sg")
        pg_r = ph.ps_ring(4, [128, 512], F32, "pg")
        pd_r = ph.ps_ring(2, [128, 512], F32, "pd")
        NTILES = NSEQ * S // TT
        pre = {}

        def do_loads(T):
            sq, s0 = divmod(T * TT, S)
            x_t, x_b = x_r.next()
            kb.dma("sp", x_t[:], xres[sq, s0:s0 + TT, :].rearrange("(j p) d -> p j d", p=128), writes=[x_b])
            return x_t, x_b

        stA = {}

        def a_stats(T):
            x_t, x_b = pre.pop(T)
            stA[T] = (x_t, x_b, nm.stats(x_t, x_b))

        def a_norm(T):
            if T not in stA:
                a_stats(T)
            x_t, x_b, st = stA[T]
            hn_t, hn_b = hn_r.next()
            nm.apply(st, x_t, x_b, gain[0], gain[1], hn_t, hn_b)
            stA[T] = (x_t, x_b, hn_t, hn_b)

        def a_tr(T):
            x_t, x_b, hn_t, hn_b = stA[T]
            hT_t, hT_b = hT_r.next()
            nm.transpose(hn_t, hn_b, 8, hT_t, hT_b, gain=gain)
            stA[T] = (x_t, x_b, hT_t, hT_b)

        pre[0] = do_loads(0)
        if NTILES > 1:
            pre[1] = do_loads(1)
        a_norm(0)
        a_tr(0)
        for T in range(NTILES):
            sq, s0 = divmod(T * TT, S)
            x_t, x_b, hT_t, hT_b = stA.pop(T)
            if T + 2 < NTILES:
                pre[T + 2] = do_loads(T + 2)
            aT_t, aT_b = aT_r.next()
            for f in range(NF):
                if f == 2 and T + 1 < NTILES:
                    a_stats(T + 1)
                if f == 8 and T + 1 < NTILES:
                    a_norm(T + 1)
                if f == 14 and T + 1 < NTILES:
                    a_tr(T + 1)
                g_t, g_b = pg_r.next()
                u_t, u_b = pg_r.next()
                for (w, p_t, p_b) in ((wg, g_t, g_b), (wu, u_t, u_b)):
                    for c in range(8):
                        kb.op("pe", lambda e, c=c, f=f, w=w, p_t=p_t: e.matmul(
                            p_t[:, 0:TT], lhsT=w[0][:, c, f * 128:(f + 1) * 128], rhs=hT_t[:, c, :],
                            start=(c == 0), stop=(c == 7)), reads=[w[1], hT_b], writes=[p_b], signal=(c == 7))
                sg_t, sg_b = sg_r.next()
                kb.op("act", lambda e, g_t=g_t, sg_t=sg_t: e.activation(out=sg_t[:], in_=g_t[:, 0:TT], func=AF.Silu),
                      reads=[g_b], writes=[sg_b])
                kb.op("dve", lambda e, f=f, u_t=u_t, sg_t=sg_t: e.tensor_tensor(
                    out=aT_t[:, f, :], in0=sg_t[:], in1=u_t[:, 0:TT], op=ALU.mult),
                    reads=[sg_b, u_b], writes=[aT_b])
            for j in range(NS):
                for n in range(2):
                    p_t, p_b = pd_r.next()
                    for f in range(NF):
                        kb.op("pe", lambda e, f=f, n=n, j=j, p_t=p_t: e.matmul(
                            p_t[:], lhsT=aT_t[:, f, j * 128:(j + 1) * 128], rhs=wd[0][:, f, n * 512:(n + 1) * 512],
                            start=(f == 0), stop=(f == NF - 1)), reads=[aT_b, wd[1]], writes=[p_b],
                            signal=(f == NF - 1))
                    kb.op("dve", lambda e, n=n, j=j, p_t=p_t: e.tensor_tensor(
                        out=x_t[:, j, n * 512:(n + 1) * 512], in0=x_t[:, j, n * 512:(n + 1) * 512], in1=p_t[:],
                        op=ALU.add), reads=[x_b, p_b], writes=[x_b])
            kb.dma("sp", xres[sq, s0:s0 + TT, :].rearrange("(j p) d -> p j d", p=128), x_t[:], reads=[x_b])


def p3c(kb, C, S, NSEQ, li, xres, gains, wpg_ap, wpp_ap, p_ap, out_ap):
    with Phase(kb) as ph:
        wpg = load_w(kb, ph, wpg_ap, D, D, "wpg")
        wpp = load_w(kb, ph, wpp_ap, PLE, D, "wpp")
        gain = ph.sb([128, 8], F32, "gfm")
        kb.dma("sp", gain[0][:], gains[1][2 * DEPTH + li], writes=[gain[1]])
        fgain = load_bcast(kb, ph, gains[0][3 * DEPTH:3 * DEPTH + 1, :], D, "fgain") if out_ap is not None else None
        nm = Norm(kb, ph, C, 4)
        x_r = ph.sb_ring(3, [128, 4, D], F32, "x")
        p_r = ph.sb_ring(3, [128, 4, PLE], F32, "p")
        pb_r = ph.sb_ring(2, [128, 4, PLE], BF16, "pb")
        pT_r = ph.sb_ring(2, [128, 2, 512], BF16, "pT")
        hn_r = ph.sb_ring(2, [128, 4, D], BF16, "hn")
        hT_r = ph.sb_ring(2, [128, 8, 512], BF16, "hT")
        th_r = ph.sb_ring(3, [128, 512], F32, "th")
        tt_r = ph.sb_ring(3, [128, 512], F32, "tt")
        o_r = ph.sb_ring(2, [128, 4, D], F32, "o") if out_ap is not None else None
        ss_r = ph.sb_ring(2, [128, 8], F32, "ss2")
        pa_r = ph.ps_ring(3, [128, 512], F32, "pa")
        pq_r = ph.ps_ring(2, [128, 512], F32, "pq")
        NTILES = NSEQ * S // 512
        pre = {}

        def do_loads(T):
            sq, s0 = divmod(T * 512, S)
            x_t, x_b = x_r.next()
            p_t, p_b = p_r.next()
            kb.dma("sp", x_t[:], xres[sq, s0:s0 + 512, :].rearrange("(j p) d -> p j d", p=128), writes=[x_b])
            kb.dma("sp", p_t[:], p_ap[sq, s0:s0 + 512, :].rearrange("(j p) d -> p j d", p=128), writes=[p_b])
            return x_t, x_b, p_t, p_b

        stA = {}

        def a_stats(T):
            x_t, x_b, p_t, p_b = pre.pop(T)
            pb_t, pb_b = pb_r.next()
            kb.op("act", lambda e: e.copy(out=pb_t[:], in_=p_t[:]), reads=[p_b], writes=[pb_b])
            stA[T] = (x_t, x_b, pb_t, pb_b, nm.stats(x_t, x_b))

        def a_norm(T):
            if T not in stA:
                a_stats(T)
            x_t, x_b, pb_t, pb_b, st = stA[T]
            hn_t, hn_b = hn_r.next()
            nm.apply(st, x_t, x_b, gain[0], gain[1], hn_t, hn_b)
            stA[T] = (x_t, x_b, pb_t, pb_b, hn_t, hn_b)

        def a_tr(T):
            x_t, x_b, pb_t, pb_b, hn_t, hn_b = stA[T]
            pT_t, pT_b = pT_r.next()
            hT_t, hT_b = hT_r.next()
            nm.transpose(hn_t, hn_b, 8, hT_t, hT_b, gain=gain)
            nm.transpose(pb_t, pb_b, 2, pT_t, pT_b, alt=1)
            stA[T] = (x_t, x_b, pT_t, pT_b, hT_t, hT_b)

        pre[0] = do_loads(0)
        if NTILES > 1:
            pre[1] = do_loads(1)
        a_norm(0)
        a_tr(0)
        for T in range(NTILES):
            sq, s0 = divmod(T * 512, S)
            x_t, x_b, pT_t, pT_b, hT_t, hT_b = stA.pop(T)
            if T + 2 < NTILES:
                pre[T + 2] = do_loads(T + 2)
            for j in range(4):
                for n in range(2):
                    gi = j * 2 + n
                    if gi == 0 and T + 1 < NTILES:
                        a_stats(T + 1)
                    if gi == 3 and T + 1 < NTILES:
                        a_norm(T + 1)
                    if gi == 5 and T + 1 < NTILES:
                        a_tr(T + 1)
                    a_t, a_b = pa_r.next()
                    q_t, q_b = pq_r.next()
                    for c in range(8):
                        kb.op("pe", lambda e, c=c, n=n, j=j, a_t=a_t: e.matmul(
                            a_t[:], lhsT=hT_t[:, c, j * 128:(j + 1) * 128], rhs=wpg[0][:, c, n * 512:(n + 1) * 512],
                            start=(c == 0), stop=(c == 7)), reads=[hT_b, wpg[1]], writes=[a_b], signal=(c == 7))
                    for c in range(2):
                        kb.op("pe", lambda e, c=c, n=n, j=j, q_t=q_t: e.matmul(
                            q_t[:], lhsT=pT_t[:, c, j * 128:(j + 1) * 128], rhs=wpp[0][:, c, n * 512:(n + 1) * 512],
                            start=(c == 0), stop=(c == 1)), reads=[pT_b, wpp[1]], writes=[q_b], signal=(c == 1))
                    th_t, th_b = th_r.next()
                    tt_t, tt_b = tt_r.next()
                    kb.op("act", lambda e, a_t=a_t, th_t=th_t: e.activation(out=th_t[:], in_=a_t[:], func=AF.Tanh,
                                                                           scale=0.5), reads=[a_b], writes=[th_b])
                    kb.op("dve", lambda e, q_t=q_t, th_t=th_t, tt_t=tt_t: e.scalar_tensor_tensor(
                        out=tt_t[:], in0=th_t[:], scalar=1.0, in1=q_t[:], op0=ALU.add, op1=ALU.mult),
                        reads=[th_b, q_b], writes=[tt_b])
                    kb.op("dve", lambda e, n=n, j=j, tt_t=tt_t: e.scalar_tensor_tensor(
                        out=x_t[:, j, n * 512:(n + 1) * 512], in0=tt_t[:], scalar=0.5,
                        in1=x_t[:, j, n * 512:(n + 1) * 512], op0=ALU.mult, op1=ALU.add),
                        reads=[tt_b, x_b], writes=[x_b])
            if out_ap is None:
                kb.dma("sp", xres[sq, s0:s0 + 512, :].rearrange("(j p) d -> p j d", p=128), x_t[:], reads=[x_b])
            else:
                o_t, o_b = o_r.next()
                ss_t, ss_b = ss_r.next()
                for j in range(4):
                    kb.op("act", lambda e, j=j: e.activation(out=nm.junk[0][:], in_=x_t[:, j, :], func=AF.Square,
                                                             accum_out=ss_t[:, j:j + 1]),
                          reads=[x_b], writes=[nm.junk[1], ss_b])
                kb.op("dve", lambda e: e.tensor_scalar(out=ss_t[:, 0:4], in0=ss_t[:, 0:4], scalar1=1.0 / D, scalar2=EPS,
                                                       op0=ALU.mult, op1=ALU.add), reads=[ss_b], writes=[ss_b])
                kb.op("pool", lambda e: e.tensor_tensor(out=ss_t[:, 4:8], in0=ss_t[:, 0:4], in1=C["mhalf"][0][:, 0:4],
                                                        op=ALU.pow), reads=[ss_b, C["mhalf"][1]], writes=[ss_b])
                for j in range(4):
                    kb.op("dve", lambda e, j=j: e.scalar_tensor_tensor(
                        out=o_t[:, j, :], in0=x_t[:, j, :], scalar=ss_t[:, 4 + j:5 + j], in1=fgain[0][:],
                        op0=ALU.mult, op1=ALU.mult), reads=[x_b, ss_b, fgain[1]], writes=[o_b])
                kb.dma("sp", out_ap[sq, s0:s0 + 512, :].rearrange("(j p) d -> p j d", p=128), o_t[:], reads=[o_b])


def host_consts(S):
    half = 128
    freq = (1.0 / (10000.0 ** np.linspace(0.0, 1.0, half, dtype=np.float32))).astype(np.float32)
    freq_t = (freq.astype(np.float64) / (2.0 * np.pi)).astype(np.float32).reshape(128, 1)
    c = np.arange(256, dtype=np.float64)
    angc = 2.0 * np.pi * np.outer(c, c) / 256.0
    cd = (np.concatenate([np.cos(angc), -np.sin(angc)], axis=1) / np.sqrt(256.0 * S)).astype(np.float32)
    s = np.arange(S, dtype=np.int64)
    m = np.outer(s, s) % S
    ang = 2.0 * np.pi * m.astype(np.float64) / S
    dft = np.stack([np.cos(ang), np.sin(ang)]).astype(np.float32).astype(ml_dtypes.bfloat16)
    HT2 = (S // 128) // 2
    SG = min(8, HT2)
    NG = HT2 // SG
    alt = np.ascontiguousarray(dft[0, S // 2:S // 2 + 1, :])
    dft = dft[:, :S // 2, :]
    dft = dft.reshape(2, NG, SG, 128, S // 512, 512).transpose(4, 0, 1, 3, 2, 5).reshape(S // 512, 2, NG, 128, SG * 512)
    dft = np.ascontiguousarray(dft)
    return freq_t, cd, dft, alt


def make_in_maps(inputs, S, NSEQ, ncores):
    f = lambda a: np.ascontiguousarray(np.asarray(a))
    freq_t, cd, dft, alt = host_consts(S)
    gains = f(np.concatenate([inputs["norm_mix"], inputs["norm_ffn"], inputs["norm_ple"],
                              np.asarray(inputs["final_norm"]).reshape(1, D)], axis=0).astype(np.float32))
    shared = {
        "gains": gains,
        "gfm": f(gains.reshape(3 * DEPTH + 1, 8, 128).transpose(0, 2, 1)),
        "ret_w_in": f(inputs["ret_w_in"]), "ret_w_out": f(inputs["ret_w_out"]), "ret_gn": f(inputs["ret_gn_gain"]),
        "ret_dl": f(np.asarray(inputs["ret_decay_logit"]).reshape(1, 16)),
        "fno_w": f(inputs["fno_w_out"]), "w_gate": f(inputs["ffn_w_gate"]), "w_up": f(inputs["ffn_w_up"]),
        "w_down": f(inputs["ffn_w_down"]), "w_pg": f(inputs["ple_w_gate"]), "w_pp": f(inputs["ple_w_proj"]),
        "freq": freq_t, "cd": cd, "dft": dft, "alt": alt,
    }
    x = np.asarray(inputs["x"])
    p = np.asarray(inputs["p"])
    pos = np.asarray(inputs["positions"]).astype(np.int32)
    maps = []
    for c in range(ncores):
        m = dict(shared)
        m["x"] = f(x[c * NSEQ:(c + 1) * NSEQ])
        m["p"] = f(p[:, c * NSEQ:(c + 1) * NSEQ])
        m["pos"] = f(pos[c * NSEQ:(c + 1) * NSEQ])
        maps.append(m)
    return maps


_NC_CACHE = {}


def kernel(**inputs):
    x = np.asarray(inputs["x"])
    B, S, _ = x.shape
    ncores = 8
    NSEQ = B // ncores
    key = (S, NSEQ)
    if key not in _NC_CACHE:
        _NC_CACHE[key] = build(S, NSEQ, [0, 1, 2, 3])
    nc = _NC_CACHE[key]
    maps = make_in_maps(inputs, S, NSEQ, ncores)
    res = run_bass_kernel_spmd(nc, maps, core_ids=list(range(ncores)))
    return np.concatenate([r["out"] for r in res.results], axis=0).astype(np.float32)
```

```python
import contextlib
import numpy as np
import ml_dtypes
import concourse.bass as bass
import concourse.mybir as mybir
from concourse.bass_utils import run_bass_kernel_spmd

F32 = mybir.dt.float32
BF16 = mybir.dt.bfloat16
I32 = mybir.dt.int32
AF = mybir.ActivationFunctionType
ALU = mybir.AluOpType

D = 1024
H = 4
DK = 256
DV = 512
VW = 2048
DFF = 2816
NF = DFF // 128
PLE = 256
DEPTH = 4
EPS = 1e-6
NB = 2
CS = 128 * NB


class Buf:
    __slots__ = ("w", "r", "name", "strict")

    def __init__(self, name="", strict=False):
        self.w = None
        self.r = {}
        self.name = name
        self.strict = strict


class KB:
    def __init__(self, nc, es, ndma=32):
        self.nc = nc
        self.es = es
        self.E = {"pe": nc.tensor, "act": nc.scalar, "dve": nc.vector, "pool": nc.gpsimd, "sp": nc.sync}
        self.sem = {e: es.enter_context(nc.semaphore("s_" + e)) for e in ("pe", "act", "dve", "pool")}
        self.cnt = {e: 0 for e in self.sem}
        self.seen = {e: {} for e in self.E}
        self.dsem = [es.enter_context(nc.semaphore("d%d" % i)) for i in range(ndma)]
        self.dcnt = [0] * ndma
        self.dring = {"sp": list(range(0, ndma - 8)), "pool": list(range(ndma - 8, ndma))}
        self.dpos = {"sp": 0, "pool": 0}
        self.uid = 0
        self.npe = 0
        self.phase_log = []

    def name(self, p):
        self.uid += 1
        return "%s_%d" % (p, self.uid)

    def _wait(self, e, tok, same=False):
        key, sem, val = tok
        if key == e and not same:
            return
        if self.seen[e].get(key, 0) >= val:
            return
        self.E[e].wait_ge(sem, val)
        self.seen[e][key] = val

    def _deps(self, e, reads, writes, same=False):
        raw_same = same or (e in ("act", "dve", "pool"))
        for b in reads:
            if b.w is not None:
                self._wait(e, b.w, raw_same)
        for b in writes:
            if b.w is not None:
                self._wait(e, b.w, same or b.strict)
            for t in b.r.values():
                self._wait(e, t, same or b.strict)

    def _mark(self, tok, reads, writes):
        for b in writes:
            b.w = tok
            b.r = {}
        for b in reads:
            old = b.r.get(tok[0])
            if old is None or old[2] < tok[2]:
                b.r[tok[0]] = tok

    def op(self, e, fn, reads=(), writes=(), signal=True):
        if e == "pe":
            self.npe += 1
        self._deps(e, reads, writes)
        ins = fn(self.E[e])
        if signal:
            self.cnt[e] += 1
            ins.then_inc(self.sem[e], 1)
            tok = (e, self.sem[e], self.cnt[e])
        else:
            tok = (e, self.sem[e], self.cnt[e] + 1)
        self._mark(tok, reads, writes)
        return ins

    def dma(self, q, out, in_, reads=(), writes=()):
        self._deps(q, reads, writes, same=True)
        ring = self.dring[q]
        k = ring[self.dpos[q] % len(ring)]
        self.dpos[q] += 1
        if self.dcnt[k] > 0:
            self._wait(q, (("d", k), self.dsem[k], 16 * self.dcnt[k]))
        ins = self.E[q].dma_start(out=out, in_=in_)
        self.dcnt[k] += 1
        ins.then_inc(self.dsem[k], 16)
        tok = (("d", k), self.dsem[k], 16 * self.dcnt[k])
        self._mark(tok, reads, writes)
        return ins

    def barrier(self, engines=None):
        for e in (engines or self.E):
            for o in self.sem:
                if o != e and self.cnt[o] > 0:
                    self._wait(e, (o, self.sem[o], self.cnt[o]))
            for k in range(len(self.dsem)):
                if self.dcnt[k] > 0:
                    self._wait(e, (("d", k), self.dsem[k], 16 * self.dcnt[k]))


class Ring:
    def __init__(self, items):
        self.items = items
        self.i = 0

    def next(self):
        it = self.items[self.i % len(self.items)]
        self.i += 1
        return it


class Phase:
    def __init__(self, kb):
        self.kb = kb
        self.es = contextlib.ExitStack()

    def __enter__(self):
        self.es.__enter__()
        return self

    def __exit__(self, *a):
        self.kb.phase_log.append(self.kb.npe)
        self.kb.barrier()
        return self.es.__exit__(*a)

    def sb(self, shape, dt, name="t"):
        t = self.es.enter_context(self.kb.nc.sbuf_tensor(self.kb.name(name), list(shape), dt))
        return t, Buf(name)

    def sb_ring(self, n, shape, dt, name="r"):
        return Ring([self.sb(shape, dt, name) for _ in range(n)])

    def ps(self, shape, dt, name="ps"):
        t = self.es.enter_context(self.kb.nc.psum_tensor(self.kb.name(name), list(shape), dt))
        return t, Buf(name)

    def ps_ring(self, n, shape, dt, name="pr"):
        return Ring([self.ps(shape, dt, name) for _ in range(n)])


def load_w(kb, ph, w_ap, kdim, ndim, name):
    nk = kdim // 128
    t, b = ph.sb([128, nk, ndim], BF16, name)
    per = max(1, min(nk, (1 << 20) // (ndim * 128 * 2)))
    for c0 in range(0, nk, per):
        c1 = min(nk, c0 + per)
        src = w_ap[c0 * 128:c1 * 128, :].rearrange("(c p) n -> p c n", p=128)
        kb.dma("pool", t[:, c0:c1, :], src, writes=[b])
    return t, b


def load_bcast(kb, ph, row_ap, n, name):
    t, b = ph.sb([128, n], F32, name)
    kb.dma("sp", t[:], row_ap.to_broadcast([128, n]), writes=[b])
    return t, b


class Norm:
    def __init__(self, kb, ph, consts, nsub):
        self.kb, self.ph, self.c, self.nsub = kb, ph, consts, nsub
        self.junk = ph.sb([128, D], F32, "junk")
        self.ss = ph.sb_ring(3, [128, 8], F32, "ss")
        self.pt = ph.ps_ring(2, [128, 1024], BF16, "pt")

    def norm(self, x_t, x_b, gain_t, gain_b, hn_t, hn_b):
        st = self.stats(x_t, x_b)
        self.apply(st, x_t, x_b, gain_t, gain_b, hn_t, hn_b)

    def stats(self, x_t, x_b):
        kb = self.kb
        ss_t, ss_b = self.ss.next()
        n = self.nsub
        for j in range(n):
            kb.op("act", lambda e, j=j: e.activation(out=self.junk[0][:], in_=x_t[:, j, :], func=AF.Square,
                                                     accum_out=ss_t[:, j:j + 1]),
                  reads=[x_b], writes=[self.junk[1], ss_b])
        kb.op("pool", lambda e: e.tensor_tensor(out=ss_t[:, 0:n], in0=ss_t[:, 0:n], in1=self.c["invd"][0][:, 0:n],
                                                op=ALU.mult), reads=[ss_b, self.c["invd"][1]], writes=[ss_b])
        kb.op("pool", lambda e: e.tensor_tensor(out=ss_t[:, 0:n], in0=ss_t[:, 0:n], in1=self.c["epsc"][0][:, 0:n],
                                                op=ALU.add), reads=[ss_b, self.c["epsc"][1]], writes=[ss_b])
        kb.op("pool", lambda e: e.tensor_tensor(out=ss_t[:, 4:4 + n], in0=ss_t[:, 0:n], in1=self.c["mhalf"][0][:, 0:n],
                                                op=ALU.pow), reads=[ss_b, self.c["mhalf"][1]], writes=[ss_b])
        return ss_t, ss_b

    def apply(self, st, x_t, x_b, gain_t, gain_b, hn_t, hn_b):
        kb = self.kb
        ss_t, ss_b = st
        for j in range(self.nsub):
            kb.op("act", lambda e, j=j: e.activation(out=hn_t[:, j, :], in_=x_t[:, j, :], func=AF.Copy,
                                                     scale=ss_t[:, 4 + j:5 + j]),
                  reads=[x_b, ss_b], writes=[hn_b])

    def transpose(self, src_t, src_b, nchunk, dst_t, dst_b, alt=0, gain=None):
        kb = self.kb
        n = self.nsub
        ident_t, ident_b = self.c["ident"]
        per = 1024 // (n * 128)
        for c0 in range(0, nchunk, per):
            pt_t, pt_b = self.pt.next()
            cn = min(per, nchunk - c0)
            for ci in range(cn):
                for j in range(n):
                    last = (ci == cn - 1 and j == n - 1)
                    kb.op("pe", lambda e, ci=ci, j=j, c0=c0: e.transpose(
                        pt_t[:, (ci * n + j) * 128:(ci * n + j + 1) * 128],
                        src_t[:, j, (c0 + ci) * 128:(c0 + ci + 1) * 128], ident_t[:]),
                        reads=[src_b, ident_b], writes=[pt_b], signal=last)
            eng = "act" if (alt + c0 // per) % 2 == 0 else "dve"
            outv = dst_t[:, c0:c0 + cn, :]
            inv = pt_t[:, 0:cn * n * 128].rearrange("p (c t) -> p c t", c=cn)
            if gain is not None:
                for ci in range(cn):
                    c = c0 + ci
                    o1 = dst_t[:, c, :]
                    i1 = pt_t[:, ci * n * 128:(ci + 1) * n * 128]
                    if (alt + c) % 2 == 0:
                        kb.op("act", lambda e, o1=o1, i1=i1, c=c: e.activation(out=o1, in_=i1, func=AF.Copy,
                                                                               scale=gain[0][:, c:c + 1]),
                              reads=[pt_b, gain[1]], writes=[dst_b])
                    else:
                        kb.op("dve", lambda e, o1=o1, i1=i1, c=c: e.tensor_scalar(out=o1, in0=i1,
                                                                                  scalar1=gain[0][:, c:c + 1],
                                                                                  scalar2=None, op0=ALU.mult),
                              reads=[pt_b, gain[1]], writes=[dst_b])
            elif eng == "act":
                kb.op("act", lambda e: e.copy(out=outv, in_=inv), reads=[pt_b], writes=[dst_b])
            else:
                kb.op("dve", lambda e: e.tensor_copy(out=outv, in_=inv), reads=[pt_b], writes=[dst_b])


def build(S, NSEQ, layers, dbg=False):
    nc = bass.Bass("TRN2", target_bir_lowering=False)
    NT = NSEQ * S
    NSC = S // CS
    ST = S // 128

    def din(name, shape, dt=F32):
        return nc.dram_tensor(name, list(shape), dt, kind="ExternalInput").ap()

    def dscr(name, shape, dt):
        return nc.dram_tensor(name, list(shape), dt, kind="ExternalOutput" if dbg else "Internal").ap()

    x_in = din("x", [NSEQ, S, D])
    p_in = din("p", [DEPTH, NSEQ, S, PLE])
    pos_in = din("pos", [NSEQ, S], I32)
    gains = din("gains", [3 * DEPTH + 1, D])
    gfm = din("gfm", [3 * DEPTH + 1, 128, 8])
    ret_w_in = din("ret_w_in", [2, D, 6144])
    ret_w_out = din("ret_w_out", [2, VW, D])
    ret_gn = din("ret_gn", [2, VW])
    ret_dl = din("ret_dl", [1, 16])
    fno_w = din("fno_w", [2, D, D])
    w_gate = din("w_gate", [DEPTH, D, DFF])
    w_up = din("w_up", [DEPTH, D, DFF])
    w_down = din("w_down", [DEPTH, DFF, D])
    w_pg = din("w_pg", [DEPTH, D, D])
    w_pp = din("w_pp", [DEPTH, PLE, D])
    freq_in = din("freq", [128, 1])
    cd_in = din("cd", [256, 512])
    _HT2 = (S // 128) // 2
    _SG = min(8, _HT2)
    dft_in = din("dft", [S // 512, 2, _HT2 // _SG, 128, _SG * 512], BF16)
    alt_in = din("alt", [1, S], BF16)
    out = nc.dram_tensor("out", [NSEQ, S, D], F32, kind="ExternalOutput").ap()

    xres = dscr("xres", [NSEQ, S, D], F32)
    cs_d = dscr("cs_d", [NSEQ, 2, 128, S], F32)
    qk_d = dscr("qk_d", [NSEQ, H, 4, 256, S], BF16)
    kt_d = dscr("kt_d", [NSEQ, 2, S, D], BF16)
    ht_d = dscr("ht_d", [D, NT], BF16)
    v_d = dscr("v_d", [NSEQ, S, VW], BF16)
    g_d = dscr("g_d", [NSEQ, S, VW], BF16)
    rb_d = dscr("rb_d", [NSEQ, H, NSC, 256, DV], BF16)
    y_d = dscr("y_d", [NSEQ, S, VW], BF16)
    ab_d = dscr("ab_d", [NSEQ, 2, S, D], BF16)
    ft_d = dscr("ft_d", [NSEQ, D, S], BF16)
    rev_d = dscr("rev_d", [NSEQ, 2, S // 2 + 1, D], BF16)

    with contextlib.ExitStack() as es:
        kb = KB(nc, es)
        g0 = Phase(kb)
        g0.__enter__()
        C = {}
        identf = g0.sb([128, 128], F32, "identf")
        C["ident"] = g0.sb([128, 128], BF16, "ident")
        C["mhalf"] = g0.sb([128, 8], F32, "mhalf")
        C["lg"] = g0.sb([128, 16], F32, "lg")
        C["freq"] = g0.sb([128, 1], F32, "freq")
        kb.op("pool", lambda e: e.memset(identf[0][:], 1.0), writes=[identf[1]])
        kb.op("pool", lambda e: e.affine_select(out=identf[0][:], in_=identf[0][:], pattern=[[-1, 128]],
                                                compare_op=ALU.is_equal, fill=0.0, base=0, channel_multiplier=1),
              reads=[identf[1]], writes=[identf[1]])
        kb.op("pool", lambda e: e.tensor_copy(out=C["ident"][0][:], in_=identf[0][:]), reads=[identf[1]],
              writes=[C["ident"][1]])
        kb.op("pool", lambda e: e.memset(C["mhalf"][0][:], -0.5), writes=[C["mhalf"][1]])
        antif = g0.sb([128, 128], F32, "antif")
        C["anti"] = g0.sb([128, 128], BF16, "anti")
        kb.op("pool", lambda e: e.memset(antif[0][:], 1.0), writes=[antif[1]])
        kb.op("pool", lambda e: e.affine_select(out=antif[0][:], in_=antif[0][:], pattern=[[1, 128]],
                                                compare_op=ALU.is_equal, fill=0.0, base=-127, channel_multiplier=1),
              reads=[antif[1]], writes=[antif[1]])
        kb.op("pool", lambda e: e.tensor_copy(out=C["anti"][0][:], in_=antif[0][:]), reads=[antif[1]],
              writes=[C["anti"][1]])
        C["invd"] = g0.sb([128, 8], F32, "invd")
        C["epsc"] = g0.sb([128, 8], F32, "epsc")
        kb.op("pool", lambda e: e.memset(C["invd"][0][:], 1.0 / D), writes=[C["invd"][1]])
        kb.op("pool", lambda e: e.memset(C["epsc"][0][:], EPS), writes=[C["epsc"][1]])
        kb.dma("sp", C["lg"][0][:], ret_dl.to_broadcast([128, 16]), writes=[C["lg"][1]])
        kb.dma("sp", C["freq"][0][:], freq_in, writes=[C["freq"][1]])
        kb.op("act", lambda e: e.activation(out=C["lg"][0][:], in_=C["lg"][0][:], func=AF.Exp, scale=-1.0),
              reads=[C["lg"][1]], writes=[C["lg"][1]])
        kb.op("act", lambda e: e.activation(out=C["lg"][0][:], in_=C["lg"][0][:], func=AF.Ln, bias=1.0),
              reads=[C["lg"][1]], writes=[C["lg"][1]])
        kb.op("dve", lambda e: e.tensor_scalar(out=C["lg"][0][:], in0=C["lg"][0][:], scalar1=-1.0, scalar2=None,
                                               op0=ALU.mult), reads=[C["lg"][1]], writes=[C["lg"][1]])

        has_ret = any(l % 2 == 0 for l in layers)
        if has_ret:
            with Phase(kb) as ph:
                W = min(S, 1024)
                pi_r = ph.sb_ring(2, [128, W], I32, "pi")
                t1_r = ph.sb_ring(2, [128, W], F32, "t1")
                ti_r = ph.sb_ring(2, [128, W], I32, "ti")
                t2_r = ph.sb_ring(2, [128, W], F32, "t2")
                t3_r = ph.sb_ring(2, [128, W], F32, "t3")
                o_r = ph.sb_ring(4, [128, W], F32, "o")
                for sq in range(NSEQ):
                    for w0 in range(0, S, W):
                        pi_t, pi_b = pi_r.next()
                        t1_t, t1_b = t1_r.next()
                        kb.dma("sp", pi_t[:], pos_in[sq:sq + 1, w0:w0 + W].to_broadcast([128, W]), writes=[pi_b])
                        kb.op("dve", lambda e: e.tensor_copy(out=t1_t[:], in_=pi_t[:]), reads=[pi_b], writes=[t1_b])
                        kb.op("dve", lambda e: e.tensor_scalar(out=t1_t[:], in0=t1_t[:], scalar1=C["freq"][0][:, 0:1],
                                                               scalar2=None, op0=ALU.mult),
                              reads=[t1_b, C["freq"][1]], writes=[t1_b])
                        for which in (1, 0):
                            ti_t, ti_b = ti_r.next()
                            t2_t, t2_b = t2_r.next()
                            t3_t, t3_b = t3_r.next()
                            o_t, o_b = o_r.next()
                            if which == 0:
                                kb.op("dve", lambda e: e.tensor_scalar(out=t1_t[:], in0=t1_t[:], scalar1=0.25,
                                                                       scalar2=None, op0=ALU.add),
                                      reads=[t1_b], writes=[t1_b])
                            kb.op("dve", lambda e: e.tensor_copy(out=ti_t[:], in_=t1_t[:]), reads=[t1_b], writes=[ti_b])
                            kb.op("dve", lambda e: e.tensor_copy(out=t2_t[:], in_=ti_t[:]), reads=[ti_b], writes=[t2_b])
                            kb.op("dve", lambda e: e.tensor_tensor(out=t3_t[:], in0=t1_t[:], in1=t2_t[:],
                                                                   op=ALU.subtract), reads=[t1_b, t2_b], writes=[t3_b])
                            kb.op("dve", lambda e: e.tensor_scalar(out=t2_t[:], in0=t3_t[:], scalar1=0.5, scalar2=None,
                                                                   op0=ALU.is_gt), reads=[t3_b], writes=[t2_b])
                            kb.op("dve", lambda e: e.tensor_tensor(out=t3_t[:], in0=t3_t[:], in1=t2_t[:],
                                                                   op=ALU.subtract), reads=[t3_b, t2_b], writes=[t3_b])
                            kb.op("dve", lambda e: e.tensor_scalar(out=t2_t[:], in0=t3_t[:], scalar1=-0.5, scalar2=None,
                                                                   op0=ALU.is_lt), reads=[t3_b], writes=[t2_b])
                            kb.op("dve", lambda e: e.tensor_tensor(out=t3_t[:], in0=t3_t[:], in1=t2_t[:], op=ALU.add),
                                  reads=[t3_b, t2_b], writes=[t3_b])
                            kb.op("act", lambda e: e.activation(out=o_t[:], in_=t3_t[:], func=AF.Sin,
                                                                scale=2.0 * np.pi), reads=[t3_b], writes=[o_b])
                            kb.dma("sp", cs_d[sq, which, :, w0:w0 + W], o_t[:], reads=[o_b])

        gains = (gains, gfm)
        for li in layers:
            lj = li // 2
            is_ret = (li % 2 == 0)
            last_layer = (li == layers[-1])
            xsrc = x_in if li == layers[0] else xres
            if is_ret:
                ret_layer(kb, C, S, NSEQ, li, lj, xsrc, cs_d, qk_d, kt_d, ht_d, v_d, g_d, rb_d, y_d,
                          gains, ret_w_in, ret_gn)
                mix_d, wmix, kmix = y_d, ret_w_out[lj], VW
            else:
                fno_layer(kb, C, S, NSEQ, li, lj, xsrc, ab_d, ft_d, gains, cd_in, dft_in, rev_d, alt_in)
                mix_d, wmix, kmix = ft_d, fno_w[lj], D
            p3a(kb, C, S, NSEQ, is_ret, mix_d, wmix, kmix, xres, xsrc)
            p3b(kb, C, S, NSEQ, li, xres, gains, w_gate[li], w_up[li], w_down[li])
            p3c(kb, C, S, NSEQ, li, xres, gains, w_pg[li], w_pp[li], p_in[li],
                out if last_layer else None)
        g0.__exit__(None, None, None)
        kb.barrier()
    nc._phase_log = kb.phase_log
    return nc


def ret_layer(kb, C, S, NSEQ, li, lj, xres, cs_d, qk_d, kt_d, ht_d, v_d, g_d, rb_d, y_d, gains, ret_w_in, ret_gn):
    NSC = S // CS
    lgc = lambda d, h: C["lg"][0][:, lj * 8 + d * 4 + h: lj * 8 + d * 4 + h + 1]
    lgb = C["lg"][1]
    with Phase(kb) as ph:
        wqk = load_w(kb, ph, ret_w_in[lj][:, 0:2048], D, 2048, "wqk")
        gain = ph.sb([128, 8], F32, "gfm")
        kb.dma("sp", gain[0][:], gains[1][li], writes=[gain[1]])
        XI = ph.sb([128, 8, 512], F32, "XI")
        ZT = ph.sb([128, 8, NB], F32, "ZT")
        it_i = ph.sb([128, 512], I32, "it_i")
        it_f = ph.sb([128, 512], F32, "it_f")
        arg = ph.sb([128, 512], F32, "arg")
        kb.op("pool", lambda e: e.iota(it_i[0][:], pattern=[[0, 512 // CS], [1, CS]], base=0, channel_multiplier=0),
              writes=[it_i[1]])
        kb.op("dve", lambda e: e.tensor_copy(out=it_f[0][:], in_=it_i[0][:]), reads=[it_i[1]], writes=[it_f[1]])
        for d in range(2):
            if d == 0:
                kb.op("dve", lambda e: e.tensor_scalar(out=arg[0][:], in0=it_f[0][:], scalar1=1.0, scalar2=None,
                                                       op0=ALU.add), reads=[it_f[1]], writes=[arg[1]])
            else:
                kb.op("dve", lambda e: e.tensor_scalar(out=arg[0][:], in0=it_f[0][:], scalar1=-1.0, scalar2=float(CS),
                                                       op0=ALU.mult, op1=ALU.add), reads=[it_f[1]], writes=[arg[1]])
            for h in range(H):
                kb.op("act", lambda e, d=d, h=h: e.activation(out=XI[0][:, d * 4 + h, :], in_=arg[0][:], func=AF.Exp,
                                                              scale=lgc(d, h)), reads=[arg[1], lgb], writes=[XI[1]])
        zi = ph.sb([128, NB], I32, "zi")
        zf = ph.sb([128, NB], F32, "zf")
        za = ph.sb([128, NB], F32, "za")
        kb.op("pool", lambda e: e.iota(zi[0][:], pattern=[[128, NB]], base=0, channel_multiplier=1), writes=[zi[1]])
        kb.op("dve", lambda e: e.tensor_copy(out=zf[0][:], in_=zi[0][:]), reads=[zi[1]], writes=[zf[1]])
        for d in range(2):
            if d == 0:
                kb.op("dve", lambda e: e.tensor_scalar(out=za[0][:], in0=zf[0][:], scalar1=-1.0, scalar2=float(CS - 1),
                                                       op0=ALU.mult, op1=ALU.add), reads=[zf[1]], writes=[za[1]])
            else:
                kb.op("dve", lambda e: e.tensor_copy(out=za[0][:], in_=zf[0][:]), reads=[zf[1]], writes=[za[1]])
            for h in range(H):
                kb.op("act", lambda e, d=d, h=h: e.activation(out=ZT[0][:, d * 4 + h, :], in_=za[0][:], func=AF.Exp,
                                                              scale=lgc(d, h)), reads=[za[1], lgb], writes=[ZT[1]])
        kb.op("dve", lambda e: e.tensor_scalar(out=ZT[0][:], in0=ZT[0][:], scalar1=1.0 / 16.0, scalar2=None,
                                               op0=ALU.mult), reads=[ZT[1]], writes=[ZT[1]])

        nm = Norm(kb, ph, C, 4)
        x_r = ph.sb_ring(2, [128, 4, D], F32, "x")
        hn_r = ph.sb_ring(1, [128, 4, D], BF16, "hn")
        hT_r = ph.sb_ring(2, [128, 8, 512], BF16, "hT")
        cs_r = ph.sb_ring(3, [128, 2, 512], F32, "cs")
        m_r = ph.sb_ring(2, [128, 4, 512], F32, "m")
        o_r = ph.sb_ring(3, [128, 2, 512], F32, "o")
        q_r = ph.sb_ring(2, [128, 3, 2, 512], BF16, "q")
        kT_r = ph.sb_ring(2, [128, 8, 512], BF16, "kT")
        kk_r = ph.sb_ring(1, [128, 2, 4, D], BF16, "kk")
        pp_r = ph.ps_ring(4, [128, 512], F32, "pp")
        pk_r = ph.ps_ring(2, [128, 1024], BF16, "pk")
        ident_t, ident_b = C["ident"]
        NTILES = NSEQ * S // 512
        pre = {}

        def do_loads(T):
            sq, s0 = divmod(T * 512, S)
            x_t, x_b = x_r.next()
            cs_t, cs_b = cs_r.next()
            kb.dma("sp", x_t[:], xres[sq, s0:s0 + 512, :].rearrange("(j p) d -> p j d", p=128), writes=[x_b])
            kb.dma("sp", cs_t[:], cs_d[sq, :, :, s0:s0 + 512].rearrange("c p s -> p c s"), writes=[cs_b])
            return x_t, x_b, cs_t, cs_b

        stA = {}

        def a_stats(T):
            x_t, x_b, cs_t, cs_b = pre.pop(T)
            stA[T] = (x_t, x_b, cs_t, cs_b, nm.stats(x_t, x_b))

        def a_norm(T):
            if T not in stA:
                a_stats(T)
            x_t, x_b, cs_t, cs_b, st = stA[T]
            hn_t, hn_b = hn_r.next()
            nm.apply(st, x_t, x_b, gain[0], gain[1], hn_t, hn_b)
            stA[T] = (cs_t, cs_b, hn_t, hn_b)

        def a_tr(T):
            cs_t, cs_b, hn_t, hn_b = stA[T]
            hT_t, hT_b = hT_r.next()
            nm.transpose(hn_t, hn_b, 8, hT_t, hT_b, gain=gain)
            kb.dma("sp", ht_d[:, T * 512:(T + 1) * 512].rearrange("(c p) t -> p c t", p=128), hT_t[:], reads=[hT_b])
            stA[T] = (cs_t, cs_b, hT_t, hT_b)

        pre[0] = do_loads(0)
        if NTILES > 1:
            pre[1] = do_loads(1)
        a_norm(0)
        a_tr(0)
        for T in range(NTILES):
            sq, s0 = divmod(T * 512, S)
            cs_t, cs_b, hT_t, hT_b = stA.pop(T)
            kT_t, kT_b = kT_r.next()
            for qk in range(2):
                for h in range(H):
                    gi = qk * 4 + h
                    if gi == 1 and T + 1 < NTILES:
                        a_stats(T + 1)
                    if gi == 3 and T + 1 < NTILES:
                        a_norm(T + 1)
                        if T + 2 < NTILES:
                            pre[T + 2] = do_loads(T + 2)
                    if gi == 6 and T + 1 < NTILES:
                        a_tr(T + 1)
                    ps = []
                    for half in range(2):
                        fc = qk * 8 + h * 2 + half
                        p_t, p_b = pp_r.next()
                        for c in range(8):
                            kb.op("pe", lambda e, c=c, fc=fc, p_t=p_t: e.matmul(
                                p_t[:], lhsT=wqk[0][:, c, fc * 128:(fc + 1) * 128], rhs=hT_t[:, c, :],
                                start=(c == 0), stop=(c == 7)), reads=[wqk[1], hT_b], writes=[p_b], signal=(c == 7))
                        ps.append((p_t, p_b))
                    m_t, m_b = m_r.next()
                    o_t, o_b = o_r.next()
                    (x1, x1b), (x2, x2b) = ps
                    cos, sin = cs_t[:, 0, :], cs_t[:, 1, :]
                    kb.op("dve", lambda e: e.tensor_tensor(out=m_t[:, 0, :], in0=x1[:], in1=cos, op=ALU.mult),
                          reads=[x1b, cs_b], writes=[m_b])
                    kb.op("dve", lambda e: e.tensor_tensor(out=m_t[:, 1, :], in0=x2[:], in1=sin, op=ALU.mult),
                          reads=[x2b, cs_b], writes=[m_b])
                    kb.op("dve", lambda e: e.tensor_tensor(out=m_t[:, 2, :], in0=x1[:], in1=sin, op=ALU.mult),
                          reads=[x1b, cs_b], writes=[m_b])
                    kb.op("dve", lambda e: e.tensor_tensor(out=m_t[:, 3, :], in0=x2[:], in1=cos, op=ALU.mult),
                          reads=[x2b, cs_b], writes=[m_b])
                    if qk == 0:
                        q_t, q_b = q_r.next()
                        kb.op("dve", lambda e: e.tensor_tensor(out=o_t[:, 0, :], in0=m_t[:, 0, :], in1=m_t[:, 1, :],
                                                               op=ALU.subtract), reads=[m_b], writes=[o_b])
                        kb.op("dve", lambda e: e.tensor_tensor(out=o_t[:, 1, :], in0=m_t[:, 2, :], in1=m_t[:, 3, :],
                                                               op=ALU.add), reads=[m_b], writes=[o_b])
                        kb.op("act", lambda e: e.copy(out=q_t[:, 0, :, :], in_=o_t[:]), reads=[o_b], writes=[q_b])
                        for d in range(2):
                            for half in range(2):
                                kb.op("pool", lambda e, d=d, half=half: e.tensor_tensor(
                                    out=q_t[:, 1 + d, half, :], in0=o_t[:, half, :], in1=XI[0][:, d * 4 + h, :],
                                    op=ALU.mult), reads=[o_b, XI[1]], writes=[q_b])
                        kb.dma("sp", qk_d[sq, h, 0:3, :, s0:s0 + 512].rearrange("k (f p) s -> p k f s", p=128),
                               q_t[:], reads=[q_b])
                    else:
                        kb.op("pool", lambda e: e.tensor_tensor(out=kT_t[:, 2 * h, :], in0=m_t[:, 0, :],
                                                                in1=m_t[:, 1, :], op=ALU.subtract),
                              reads=[m_b], writes=[kT_b])
                        kb.op("pool", lambda e: e.tensor_tensor(out=kT_t[:, 2 * h + 1, :], in0=m_t[:, 2, :],
                                                                in1=m_t[:, 3, :], op=ALU.add),
                              reads=[m_b], writes=[kT_b])
            for h in range(H):
                kb.dma("sp", qk_d[sq, h, 3, :, s0:s0 + 512].rearrange("(f p) s -> p f s", p=128),
                       kT_t[:, 2 * h:2 * h + 2, :], reads=[kT_b])
            kk_t, kk_b = kk_r.next()
            for j in range(4):
                pk_t, pk_b = pk_r.next()
                for c in range(8):
                    kb.op("pe", lambda e, c=c, j=j: e.transpose(pk_t[:, c * 128:(c + 1) * 128],
                                                                kT_t[:, c, j * 128:(j + 1) * 128], ident_t[:]),
                          reads=[kT_b, ident_b], writes=[pk_b], signal=(c == 7))
                blk = ((s0 // 128) + j) % NB
                for d in range(2):
                    for h in range(H):
                        kb.op("act", lambda e, d=d, h=h, j=j: e.activation(
                            out=kk_t[:, d, j, h * 256:(h + 1) * 256], in_=pk_t[:, h * 256:(h + 1) * 256],
                            func=AF.Copy, scale=ZT[0][:, d * 4 + h, blk:blk + 1]),
                            reads=[pk_b, ZT[1]], writes=[kk_b])
            for d in range(2):
                kb.dma("sp", kt_d[sq, d, s0:s0 + 512, :].rearrange("(j p) f -> p j f", p=128), kk_t[:, d, :, :],
                       reads=[kk_b])

    with Phase(kb) as ph:
        wvg = load_w(kb, ph, ret_w_in[lj][:, 2048:6144], D, 4096, "wvg")
        gng = load_bcast(kb, ph, ret_gn[lj:lj + 1, :], VW, "gng")
        hT_r = ph.sb_ring(2, [128, 8, 512], BF16, "hT")
        v_r = ph.sb_ring(2, [128, 4, VW], BF16, "v")
        g_r = ph.sb_ring(2, [128, 4, VW], BF16, "g")
        sg_r = ph.sb_ring(3, [128, 512], F32, "sg")
        pp_r = ph.ps_ring(6, [128, 512], F32, "pp")
        NTILES = NSEQ * S // 512
        pre = {}

        def do_loads(T):
            hT_t, hT_b = hT_r.next()
            kb.dma("sp", hT_t[:], ht_d[:, T * 512:(T + 1) * 512].rearrange("(c p) t -> p c t", p=128), writes=[hT_b])
            return hT_t, hT_b

        for T in range(NTILES):
            sq, s0 = divmod(T * 512, S)
            if T == 0:
                pre[0] = do_loads(0)
            if T + 1 < NTILES:
                pre[T + 1] = do_loads(T + 1)
            hT_t, hT_b = pre.pop(T)
            v_t, v_b = v_r.next()
            g_t, g_b = g_r.next()
            for j in range(4):
                for n in range(8):
                    p_t, p_b = pp_r.next()
                    for c in range(8):
                        kb.op("pe", lambda e, c=c, n=n, j=j, p_t=p_t: e.matmul(
                            p_t[:], lhsT=hT_t[:, c, j * 128:(j + 1) * 128], rhs=wvg[0][:, c, n * 512:(n + 1) * 512],
                            start=(c == 0), stop=(c == 7)), reads=[wvg[1], hT_b], writes=[p_b], signal=(c == 7))
                    if n < 4:
                        if n % 2 == 0:
                            kb.op("act", lambda e, n=n, j=j, p_t=p_t: e.copy(out=v_t[:, j, n * 512:(n + 1) * 512],
                                                                             in_=p_t[:]), reads=[p_b], writes=[v_b])
                        else:
                            kb.op("dve", lambda e, n=n, j=j, p_t=p_t: e.tensor_copy(
                                out=v_t[:, j, n * 512:(n + 1) * 512], in_=p_t[:]), reads=[p_b], writes=[v_b])
                    else:
                        sg_t, sg_b = sg_r.next()
                        kb.op("act", lambda e, p_t=p_t, sg_t=sg_t: e.activation(out=sg_t[:], in_=p_t[:], func=AF.Silu),
                              reads=[p_b], writes=[sg_b])
                        kb.op("pool", lambda e, n=n, j=j, sg_t=sg_t: e.tensor_tensor(
                            out=g_t[:, j, (n - 4) * 512:(n - 3) * 512], in0=sg_t[:],
                            in1=gng[0][:, (n - 4) * 512:(n - 3) * 512], op=ALU.mult),
                            reads=[sg_b, gng[1]], writes=[g_b])
            kb.dma("sp", v_d[sq, s0:s0 + 512, :].rearrange("(j p) f -> p j f", p=128), v_t[:], reads=[v_b])
            kb.dma("sp", g_d[sq, s0:s0 + 512, :].rearrange("(j p) f -> p j f", p=128), g_t[:], reads=[g_b])

    with Phase(kb) as ph:
        NO = 2 * NB - 1
        MK = ph.sb([128, H, NO, 128], F32, "MK")
        GG = ph.sb([128, 8], F32, "GG")
        csz = ph.sb([128, 1], F32, "csz")
        kb.op("pool", lambda e: e.memset(csz[0][:], float(CS)), writes=[csz[1]])
        for d in range(2):
            for h in range(H):
                kb.op("act", lambda e, d=d, h=h: e.activation(out=GG[0][:, d * 4 + h:d * 4 + h + 1], in_=csz[0][:],
                                                              func=AF.Exp, scale=lgc(d, h)),
                      reads=[csz[1], lgb], writes=[GG[1]])
        di = ph.sb([128, 128], I32, "di")
        df = ph.sb([128, 128], F32, "df")
        mpos = ph.sb([128, 128], F32, "mpos")
        mneg = ph.sb([128, 128], F32, "mneg")
        ind = ph.sb([128, 128], F32, "ind")
        inb = ph.sb([128, 128], F32, "inb")
        ef = ph.sb([128, 128], F32, "ef")
        eb = ph.sb([128, 128], F32, "eb")
        for oi in range(NO):
            off = oi - (NB - 1)
            kb.op("pool", lambda e: e.iota(di[0][:], pattern=[[1, 128]], base=-128 * off, channel_multiplier=-1),
                  reads=[di[1]], writes=[di[1]])
            kb.op("dve", lambda e: e.tensor_copy(out=df[0][:], in_=di[0][:]), reads=[di[1]], writes=[df[1]])
            kb.op("dve", lambda e: e.tensor_scalar(out=mpos[0][:], in0=df[0][:], scalar1=0.0, scalar2=None,
                                                   op0=ALU.max), reads=[df[1]], writes=[mpos[1]])
            kb.op("dve", lambda e: e.tensor_scalar(out=mneg[0][:], in0=df[0][:], scalar1=-1.0, scalar2=0.0,
                                                   op0=ALU.mult, op1=ALU.max), reads=[df[1]], writes=[mneg[1]])
            kb.op("dve", lambda e: e.tensor_scalar(out=ind[0][:], in0=df[0][:], scalar1=0.0, scalar2=1.0 / 16.0,
                                                   op0=ALU.is_ge, op1=ALU.mult), reads=[df[1]], writes=[ind[1]])
            kb.op("dve", lambda e: e.tensor_scalar(out=inb[0][:], in0=df[0][:], scalar1=0.0, scalar2=1.0 / 16.0,
                                                   op0=ALU.is_lt, op1=ALU.mult), reads=[df[1]], writes=[inb[1]])
            for h in range(H):
                kb.op("act", lambda e, h=h: e.activation(out=ef[0][:], in_=mpos[0][:], func=AF.Exp, scale=lgc(0, h)),
                      reads=[mpos[1], lgb], writes=[ef[1]])
                kb.op("act", lambda e, h=h: e.activation(out=eb[0][:], in_=mneg[0][:], func=AF.Exp, scale=lgc(1, h)),
                      reads=[mneg[1], lgb], writes=[eb[1]])
                kb.op("dve", lambda e: e.tensor_tensor(out=ef[0][:], in0=ef[0][:], in1=ind[0][:], op=ALU.mult),
                      reads=[ef[1], ind[1]], writes=[ef[1]])
                kb.op("dve", lambda e: e.tensor_tensor(out=eb[0][:], in0=eb[0][:], in1=inb[0][:], op=ALU.mult),
                      reads=[eb[1], inb[1]], writes=[eb[1]])
                kb.op("dve", lambda e, h=h, oi=oi: e.tensor_tensor(out=MK[0][:, h, oi, :], in0=ef[0][:], in1=eb[0][:],
                                                                   op=ALU.add), reads=[ef[1], eb[1]], writes=[MK[1]])

        R32 = [ph.sb([128, 2, DV], F32, "R32") for _ in range(H)]
        Rbf = [ph.sb_ring(2, [128, 2, DV], BF16, "Rbf") for _ in range(H)]
        kk_r = ph.sb_ring(6, [128, NB, 256], BF16, "kk")
        v_r = ph.sb_ring(6, [128, NB, DV], BF16, "v")
        g_r = ph.sb_ring(6, [128, NB, DV], BF16, "g")
        qk_r = ph.sb_ring(6, [128, 4, 2, CS], BF16, "qk")
        rb_r = ph.sb_ring(6, [128, 2, DV], BF16, "rb")
        sT_r = ph.sb_ring(3, [128, NB, 128], BF16, "sT")
        yn_r = ph.sb_ring(3, [128, DV], F32, "yn")
        y_r = ph.sb_ring(4, [128, NB, DV], BF16, "y")
        st_r = ph.sb_ring(6, [128, 8], F32, "st")
        junk = ph.sb([128, DV], F32, "junk2")
        ps_s = ph.ps_ring(2, [128, 512], F32, "ps_s")
        ps_o = ph.ps_ring(4, [128, 512], F32, "ps_o")
        ps_r = ph.ps_ring(2, [128, 512], F32, "ps_r")

        def state_update(h, d, kk_t, kk_b, v_t, v_b):
            r32_t, r32_b = R32[h]
            nb_t, nb_b = Rbf[h].next()
            for half in range(2):
                p_t, p_b = ps_r.next()
                for blk in range(NB):
                    kb.op("pe", lambda e, blk=blk, half=half, p_t=p_t: e.matmul(
                        p_t[:], lhsT=kk_t[:, blk, half * 128:(half + 1) * 128], rhs=v_t[:, blk, :],
                        start=(blk == 0), stop=(blk == NB - 1)), reads=[kk_b, v_b], writes=[p_b],
                        signal=(blk == NB - 1))
                kb.op("dve", lambda e, half=half, p_t=p_t: e.scalar_tensor_tensor(
                    out=r32_t[:, half, :], in0=r32_t[:, half, :], scalar=GG[0][:, d * 4 + h:d * 4 + h + 1],
                    in1=p_t[:], op0=ALU.mult, op1=ALU.add), reads=[r32_b, p_b, GG[1]], writes=[r32_b])
            kb.op("act", lambda e: e.copy(out=nb_t[:], in_=r32_t[:]), reads=[r32_b], writes=[nb_b])
            return nb_t, nb_b

        rbufs = {(sq, h, sc): Buf("rbd") for sq in range(NSEQ) for h in range(H) for sc in range(NSC)}
        for sq in range(NSEQ):
            cur = []
            for h in range(H):
                kb.op("pool", lambda e, h=h: e.memset(R32[h][0][:], 0.0), writes=[R32[h][1]])
                nb_t, nb_b = Rbf[h].next()
                kb.op("pool", lambda e, nb_t=nb_t: e.memset(nb_t[:], 0.0), writes=[nb_b])
                cur.append((nb_t, nb_b))
            def b_loads(it):
                scr, h = divmod(it, H)
                sc = NSC - 1 - scr
                s0 = sc * CS
                kk_t, kk_b = kk_r.next()
                v_t, v_b = v_r.next()
                kb.dma("sp", kk_t[:], kt_d[sq, 1, s0:s0 + CS, h * 256:(h + 1) * 256].rearrange(
                    "(b p) f -> p b f", p=128), writes=[kk_b])
                kb.dma("sp", v_t[:], v_d[sq, s0:s0 + CS, h * DV:(h + 1) * DV].rearrange("(b p) f -> p b f", p=128),
                       writes=[v_b])
                return kk_t, kk_b, v_t, v_b

            NITB = (NSC - 1) * H
            PFB = 4
            bpre = {}
            for it in range(min(PFB, NITB)):
                bpre[it] = b_loads(it)
            for scr in range(NSC):
                sc = NSC - 1 - scr
                for h in range(H):
                    kb.dma("sp", rb_d[sq, h, sc].rearrange("(f p) e -> p f e", p=128), cur[h][0][:], reads=[cur[h][1]],
                           writes=[rbufs[(sq, h, sc)]])
                    if sc == 0:
                        continue
                    it = scr * H + h
                    if it + PFB < NITB:
                        bpre[it + PFB] = b_loads(it + PFB)
                    kk_t, kk_b, v_t, v_b = bpre.pop(it)
                    cur[h] = state_update(h, 1, kk_t, kk_b, v_t, v_b)
            cur = []
            for h in range(H):
                kb.op("pool", lambda e, h=h: e.memset(R32[h][0][:], 0.0), reads=[], writes=[R32[h][1]])
                nb_t, nb_b = Rbf[h].next()
                kb.op("pool", lambda e, nb_t=nb_t: e.memset(nb_t[:], 0.0), writes=[nb_b])
                cur.append((nb_t, nb_b))
            def f_loads(it):
                sc, h = divmod(it, H)
                s0 = sc * CS
                kk_t, kk_b = kk_r.next()
                v_t, v_b = v_r.next()
                g_t, g_b = g_r.next()
                qk_t, qk_b = qk_r.next()
                rb_t, rb_b = rb_r.next()
                kb.dma("sp", qk_t[:], qk_d[sq, h, :, :, s0:s0 + CS].rearrange("k (f p) s -> p k f s", p=128),
                       writes=[qk_b])
                kb.dma("sp", kk_t[:], kt_d[sq, 0, s0:s0 + CS, h * 256:(h + 1) * 256].rearrange(
                    "(b p) f -> p b f", p=128), writes=[kk_b])
                kb.dma("sp", v_t[:], v_d[sq, s0:s0 + CS, h * DV:(h + 1) * DV].rearrange("(b p) f -> p b f", p=128),
                       writes=[v_b])
                kb.dma("sp", g_t[:], g_d[sq, s0:s0 + CS, h * DV:(h + 1) * DV].rearrange("(b p) f -> p b f", p=128),
                       writes=[g_b])
                kb.dma("sp", rb_t[:], rb_d[sq, h, sc].rearrange("(f p) e -> p f e", p=128), reads=[rbufs[(sq, h, sc)]],
                       writes=[rb_b])
                return kk_t, kk_b, v_t, v_b, g_t, g_b, qk_t, qk_b, rb_t, rb_b

            NIT = NSC * H
            PF = 3
            fpre = {}
            for it in range(min(PF, NIT)):
                fpre[it] = f_loads(it)
            backq = []
            SKEW = 2
            for sc in range(NSC):
                s0 = sc * CS
                for h in range(H):
                    it = sc * H + h
                    if it + PF < NIT:
                        fpre[it + PF] = f_loads(it + PF)
                    kk_t, kk_b, v_t, v_b, g_t, g_b, qk_t, qk_b, rb_t, rb_b = fpre.pop(it)
                    rf_t, rf_b = cur[h]
                    y_t, y_b = y_r.next()
                    for ib in range(NB):
                        isl = slice(ib * 128, (ib + 1) * 128)
                        s_t, s_b = ps_s.next()
                        sT_t, sT_b = sT_r.next()
                        for jb in range(NB):
                            for half in range(2):
                                kb.op("pe", lambda e, jb=jb, half=half: e.matmul(
                                    s_t[:, jb * 128:(jb + 1) * 128], lhsT=qk_t[:, 3, half, jb * 128:(jb + 1) * 128],
                                    rhs=qk_t[:, 0, half, isl], start=(half == 0), stop=(half == 1)),
                                    reads=[qk_b], writes=[s_b], signal=(half == 1 and jb == NB - 1))
                        for jb in range(NB):
                            kb.op("dve", lambda e, jb=jb: e.tensor_tensor(
                                out=sT_t[:, jb, :], in0=s_t[:, jb * 128:(jb + 1) * 128],
                                in1=MK[0][:, h, jb - ib + NB - 1, :], op=ALU.mult),
                                reads=[s_b, MK[1]], writes=[sT_b])
                        o_t, o_b = ps_o.next()
                        nmm = NB + 4
                        k = 0
                        for half in range(2):
                            kb.op("pe", lambda e, half=half, k=k: e.matmul(
                                o_t[:], lhsT=qk_t[:, 1, half, isl], rhs=rf_t[:, half, :], start=(k == 0), stop=False),
                                reads=[qk_b, rf_b], writes=[o_b], signal=False)
                            k += 1
                        for half in range(2):
                            kb.op("pe", lambda e, half=half: e.matmul(
                                o_t[:], lhsT=qk_t[:, 2, half, isl], rhs=rb_t[:, half, :], start=False, stop=False),
                                reads=[qk_b, rb_b], writes=[o_b], signal=False)
                            k += 1
                        for jb in range(NB):
                            kb.op("pe", lambda e, jb=jb: e.matmul(
                                o_t[:], lhsT=sT_t[:, jb, :], rhs=v_t[:, jb, :], start=False, stop=(jb == NB - 1)),
                                reads=[sT_b, v_b], writes=[o_b], signal=(jb == NB - 1))
                        def back(o_t=o_t, o_b=o_b, ib=ib, y_t=y_t, y_b=y_b, g_t=g_t, g_b=g_b, h=h, s0=s0):
                            st_t, st_b = st_r.next()
                            yn_t, yn_b = yn_r.next()
                            kb.op("act", lambda e: e.activation(out=junk[0][:], in_=o_t[:], func=AF.Identity,
                                                                accum_out=st_t[:, 0:1]),
                                  reads=[o_b], writes=[junk[1], st_b])
                            kb.op("act", lambda e: e.activation(out=junk[0][:], in_=o_t[:], func=AF.Square,
                                                                accum_out=st_t[:, 1:2]),
                                  reads=[o_b], writes=[junk[1], st_b])
                            kb.op("dve", lambda e: e.tensor_scalar(out=st_t[:, 2:4], in0=st_t[:, 0:2], scalar1=1.0 / DV,
                                                                   scalar2=None, op0=ALU.mult), reads=[st_b], writes=[st_b])
                            kb.op("dve", lambda e: e.tensor_tensor(out=st_t[:, 4:5], in0=st_t[:, 2:3], in1=st_t[:, 2:3],
                                                                   op=ALU.mult), reads=[st_b], writes=[st_b])
                            kb.op("dve", lambda e: e.scalar_tensor_tensor(out=st_t[:, 5:6], in0=st_t[:, 3:4], scalar=EPS,
                                                                          in1=st_t[:, 4:5], op0=ALU.add, op1=ALU.subtract),
                                  reads=[st_b], writes=[st_b])
                            kb.op("pool", lambda e: e.tensor_tensor(out=st_t[:, 6:7], in0=st_t[:, 5:6],
                                                                    in1=C["mhalf"][0][:, 0:1], op=ALU.pow),
                                  reads=[st_b, C["mhalf"][1]], writes=[st_b])
                            kb.op("dve", lambda e: e.scalar_tensor_tensor(out=st_t[:, 7:8], in0=st_t[:, 2:3], scalar=-1.0,
                                                                          in1=st_t[:, 6:7], op0=ALU.mult, op1=ALU.mult),
                                  reads=[st_b], writes=[st_b])
                            kb.op("act", lambda e: e.activation(out=yn_t[:], in_=o_t[:], func=AF.Identity,
                                                                scale=st_t[:, 6:7], bias=st_t[:, 7:8]),
                                  reads=[o_b, st_b], writes=[yn_b])
                            kb.op("pool", lambda e, ib=ib: e.tensor_tensor(out=y_t[:, ib, :], in0=yn_t[:], in1=g_t[:, ib, :],
                                                                           op=ALU.mult), reads=[yn_b, g_b], writes=[y_b])
                            if ib == NB - 1:
                                kb.dma("pool", y_d[sq, s0:s0 + CS, h * DV:(h + 1) * DV].rearrange("(b p) f -> p b f", p=128), y_t[:],
                                       reads=[y_b])
                        backq.append(back)
                        while len(backq) > SKEW:
                            backq.pop(0)()
                    if sc < NSC - 1:
                        cur[h] = state_update(h, 0, kk_t, kk_b, v_t, v_b)
            while backq:
                backq.pop(0)()


def fno_layer(kb, C, S, NSEQ, li, lj, xres, ab_d, ft_d, gains, cd_in, dft_in, rev_d, alt_in):
    ST = S // 128
    with Phase(kb) as ph:
        cd = load_w(kb, ph, cd_in, 256, 512, "cd")
        gain = ph.sb([128, 8], F32, "gfm")
        kb.dma("sp", gain[0][:], gains[1][li], writes=[gain[1]])
        nm = Norm(kb, ph, C, 4)
        x_r = ph.sb_ring(2, [128, 4, D], F32, "x")
        hn_r = ph.sb_ring(2, [128, 4, D], BF16, "hn")
        hT_r = ph.sb_ring(2, [128, 8, 512], BF16, "hT")
        ab_r = ph.sb_ring(2, [128, 2, 4, D], BF16, "ab")
        rv_r = ph.sb_ring(3, [128, 2, D], BF16, "rv")
        pp_r = ph.ps_ring(4, [128, 512], F32, "pp")
        zr = ph.sb([1, D], BF16, "zr")
        kb.op("pool", lambda e: e.memset(zr[0][:], 0.0), writes=[zr[1]])
        for sq_ in range(NSEQ):
            for a_ in range(2):
                kb.dma("sp", rev_d[sq_, a_, 0:1, :], zr[0][:], reads=[zr[1]])
        NTILES = NSEQ * S // 512
        pre = {}

        def do_loads(T):
            sq, s0 = divmod(T * 512, S)
            x_t, x_b = x_r.next()
            kb.dma("sp", x_t[:], xres[sq, s0:s0 + 512, :].rearrange("(j p) d -> p j d", p=128), writes=[x_b])
            return x_t, x_b

        stA = {}

        def a_stats(T):
            x_t, x_b = pre.pop(T)
            stA[T] = (x_t, x_b, nm.stats(x_t, x_b))

        def a_norm(T):
            if T not in stA:
                a_stats(T)
            x_t, x_b, st = stA[T]
            hn_t, hn_b = hn_r.next()
            nm.apply(st, x_t, x_b, None, None, hn_t, hn_b)
            stA[T] = (hn_t, hn_b)

        def a_tr(T):
            hn_t, hn_b = stA[T]
            hT_t, hT_b = hT_r.next()
            nm.transpose(hn_t, hn_b, 8, hT_t, hT_b, gain=gain)
            stA[T] = (hT_t, hT_b)

        pre[0] = do_loads(0)
        a_norm(0)
        a_tr(0)
        if NTILES > 1:
            pre[1] = do_loads(1)
        for T in range(NTILES):
            sq, s0 = divmod(T * 512, S)
            hT_t, hT_b = stA.pop(T)
            ab_t, ab_b = ab_r.next()
            for j in range(4):
                for g in range(4):
                    gi = j * 4 + g
                    if gi == 0 and T + 1 < NTILES:
                        a_stats(T + 1)
                    if gi == 5 and T + 1 < NTILES:
                        a_norm(T + 1)
                        if T + 2 < NTILES:
                            pre[T + 2] = do_loads(T + 2)
                    if gi == 11 and T + 1 < NTILES:
                        a_tr(T + 1)
                    p_t, p_b = pp_r.next()
                    for half in range(2):
                        kb.op("pe", lambda e, half=half, g=g, j=j, p_t=p_t: e.matmul(
                            p_t[:], lhsT=hT_t[:, 2 * g + half, j * 128:(j + 1) * 128], rhs=cd[0][:, half, :],
                            start=(half == 0), stop=(half == 1)), reads=[hT_b, cd[1]], writes=[p_b],
                            signal=(half == 1))
                    outv = ab_t[:, :, j, g * 256:(g + 1) * 256]
                    inv = p_t[:].rearrange("p (a l) -> p a l", a=2)
                    if (j * 4 + g) % 2 == 0:
                        kb.op("act", lambda e, outv=outv, inv=inv: e.copy(out=outv, in_=inv), reads=[p_b], writes=[ab_b])
                    else:
                        kb.op("dve", lambda e, outv=outv, inv=inv: e.tensor_copy(out=outv, in_=inv), reads=[p_b],
                              writes=[ab_b])
            for a in range(2):
                kb.dma("sp", ab_d[sq, a, s0:s0 + 512, :].rearrange("(j p) f -> p j f", p=128), ab_t[:, a, :, :],
                       reads=[ab_b])
            for j in range(4):
                t = s0 // 128 + j
                if t < ST // 2:
                    continue
                rv_t, rv_b = rv_r.next()
                for a in range(2):
                    for n in range(2):
                        p_t, p_b = pp_r.next()
                        kb.op("pe", lambda e, a=a, n=n, j=j, p_t=p_t: e.matmul(
                            p_t[:], lhsT=C["anti"][0][:], rhs=ab_t[:, a, j, n * 512:(n + 1) * 512], start=True, stop=True),
                            reads=[ab_b, C["anti"][1]], writes=[p_b])
                        if (a + n) % 2 == 0:
                            kb.op("act", lambda e, a=a, n=n, p_t=p_t: e.copy(out=rv_t[:, a, n * 512:(n + 1) * 512],
                                                                             in_=p_t[:]), reads=[p_b], writes=[rv_b])
                        else:
                            kb.op("dve", lambda e, a=a, n=n, p_t=p_t: e.tensor_copy(
                                out=rv_t[:, a, n * 512:(n + 1) * 512], in_=p_t[:]), reads=[p_b], writes=[rv_b])
                r0 = (ST - 1 - t) * 128 + 1
                for a in range(2):
                    kb.dma("sp", rev_d[sq, a, r0:r0 + 128, :], rv_t[:, a, :], reads=[rv_b])
    with Phase(kb) as ph:
        HT2 = ST // 2
        SG = min(8, HT2)
        NG = HT2 // SG
        L_r = ph.sb_ring(2, [128, 2, HT2, 512], BF16, "L")
        M_r = ph.sb_ring(2, [128, 2, HT2, 512], BF16, "M")
        mid_r = ph.sb_ring(2, [1, 512], BF16, "mid")
        alt = ph.sb([1, S], BF16, "alt")
        kb.dma("sp", alt[0][:], alt_in, writes=[alt[1]])
        pc_r = ph.sb_ring(6, [128, SG, 512], BF16, "pc")
        fo_r = ph.sb_ring(2, [128, 4, 512], BF16, "fo")
        pp_r = ph.ps_ring(8, [128, 512], F32, "pp")
        def f2_prep(sq, chh):
                L_t, L_b = L_r.next()
                M_t, M_b = M_r.next()
                mid_t, mid_b = mid_r.next()
                csl = slice(chh * 512, (chh + 1) * 512)
                for a in range(2):
                    for s1 in range(0, HT2, 8):
                        s2 = min(HT2, s1 + 8)
                        kb.dma("sp", L_t[:, a, s1:s2, :],
                               ab_d[sq, a, s1 * 128:s2 * 128, csl].rearrange("(t p) c -> p t c", p=128), writes=[L_b])
                        kb.dma("sp", M_t[:, a, s1:s2, :],
                               rev_d[sq, a, s1 * 128:s2 * 128, csl].rearrange("(t p) c -> p t c", p=128), writes=[M_b])
                kb.dma("sp", mid_t[:], rev_d[sq, 0, S // 2:S // 2 + 1, csl], writes=[mid_b])
                kb.op("dve", lambda e: e.tensor_tensor(out=L_t[:, 0, :, :], in0=L_t[:, 0, :, :], in1=M_t[:, 0, :, :],
                                                       op=ALU.add), reads=[L_b, M_b], writes=[L_b])
                kb.op("dve", lambda e: e.tensor_tensor(out=L_t[:, 1, :, :], in0=L_t[:, 1, :, :], in1=M_t[:, 1, :, :],
                                                       op=ALU.subtract), reads=[L_b, M_b], writes=[L_b])
                return L_t, L_b, mid_t, mid_b

        jobs = [(sq, chh) for sq in range(NSEQ) for chh in range(2)]
        prep = {0: f2_prep(*jobs[0])}
        for ji, (sq, chh) in enumerate(jobs):
                L_t, L_b, mid_t, mid_b = prep.pop(ji)
                for k0 in range(0, S, 512):
                    if k0 == min(512, S - 512) and ji + 1 < len(jobs):
                        prep[ji + 1] = f2_prep(*jobs[ji + 1])
                    banks = [pp_r.next() for _ in range(4)]
                    n_acc = 2 * HT2 + 1
                    for cc in range(4):
                        p_t, p_b = banks[cc]
                        kb.op("pe", lambda e, cc=cc, p_t=p_t: e.matmul(
                            p_t[:], lhsT=mid_t[0:1, cc * 128:(cc + 1) * 128], rhs=alt[0][0:1, k0:k0 + 512],
                            start=True, stop=False), reads=[mid_b, alt[1]], writes=[p_b], signal=False)
                    idx = 1
                    for g in range(NG):
                        for a in range(2):
                            pc_t, pc_b = pc_r.next()
                            kb.dma("sp", pc_t[:], dft_in[k0 // 512, a, g].rearrange("p (t k) -> p t k", t=SG),
                                   writes=[pc_b])
                            for t in range(SG):
                                for cc in range(4):
                                    p_t, p_b = banks[cc]
                                    kb.op("pe", lambda e, t=t, cc=cc, a=a, g=g, p_t=p_t, idx=idx: e.matmul(
                                        p_t[:], lhsT=L_t[:, a, g * SG + t, cc * 128:(cc + 1) * 128], rhs=pc_t[:, t, :],
                                        start=False, stop=(idx == n_acc - 1)),
                                        reads=[L_b, pc_b], writes=[p_b], signal=(idx == n_acc - 1 or t == SG - 1))
                                idx += 1
                    fo_t, fo_b = fo_r.next()
                    for cc in range(4):
                        p_t, p_b = banks[cc]
                        if cc % 2 == 0:
                            kb.op("act", lambda e, cc=cc, p_t=p_t: e.copy(out=fo_t[:, cc, :], in_=p_t[:]), reads=[p_b],
                                  writes=[fo_b])
                        else:
                            kb.op("dve", lambda e, cc=cc, p_t=p_t: e.tensor_copy(out=fo_t[:, cc, :], in_=p_t[:]),
                                  reads=[p_b], writes=[fo_b])
                    kb.dma("pool", ft_d[sq, chh * 512:(chh + 1) * 512, k0:k0 + 512].rearrange("(c p) k -> p c k", p=128),
                           fo_t[:], reads=[fo_b])


def p3a(kb, C, S, NSEQ, is_ret, mix_d, wmix_ap, kmix, xres, xsrc):
    nkc = kmix // 128
    with Phase(kb) as ph:
        wo = load_w(kb, ph, wmix_ap, kmix, D, "wo")
        nm = Norm(kb, ph, C, 4)
        x_r = ph.sb_ring(3, [128, 4, D], F32, "x")
        y_r = ph.sb_ring(3, [128, 4, kmix], BF16, "y") if is_ret else None
        mT_r = ph.sb_ring(2 if is_ret else 3, [128, nkc, 512], BF16, "mT")
        pp_r = ph.ps_ring(4, [128, 512], F32, "pp")
        NTILES = NSEQ * S // 512
        pre = {}

        def do_loads(T):
            sq, s0 = divmod(T * 512, S)
            x_t, x_b = x_r.next()
            kb.dma("sp", x_t[:], xsrc[sq, s0:s0 + 512, :].rearrange("(j p) d -> p j d", p=128), writes=[x_b])
            if is_ret:
                y_t, y_b = y_r.next()
                kb.dma("sp", y_t[:], mix_d[sq, s0:s0 + 512, :].rearrange("(j p) f -> p j f", p=128), writes=[y_b])
                return x_t, x_b, y_t, y_b
            mT_t, mT_b = mT_r.next()
            kb.dma("sp", mT_t[:], mix_d[sq, :, s0:s0 + 512].rearrange("(c p) k -> p c k", p=128), writes=[mT_b])
            return x_t, x_b, mT_t, mT_b

        stA = {}

        def a_tr(T):
            if is_ret:
                x_t, x_b, y_t, y_b = pre.pop(T)
                mT_t, mT_b = mT_r.next()
                nm.transpose(y_t, y_b, nkc, mT_t, mT_b)
                stA[T] = (x_t, x_b, mT_t, mT_b)
            else:
                stA[T] = pre.pop(T)

        pre[0] = do_loads(0)
        if NTILES > 1:
            pre[1] = do_loads(1)
        a_tr(0)
        for T in range(NTILES):
            sq, s0 = divmod(T * 512, S)
            x_t, x_b, mT_t, mT_b = stA.pop(T)
            if T + 2 < NTILES:
                pre[T + 2] = do_loads(T + 2)
            for j in range(4):
                for n in range(2):
                    if j == 2 and n == 0 and T + 1 < NTILES:
                        a_tr(T + 1)
                    p_t, p_b = pp_r.next()
                    for c in range(nkc):
                        kb.op("pe", lambda e, c=c, n=n, j=j, p_t=p_t: e.matmul(
                            p_t[:], lhsT=mT_t[:, c, j * 128:(j + 1) * 128], rhs=wo[0][:, c, n * 512:(n + 1) * 512],
                            start=(c == 0), stop=(c == nkc - 1)), reads=[mT_b, wo[1]], writes=[p_b],
                            signal=(c == nkc - 1))
                    kb.op("dve", lambda e, n=n, j=j, p_t=p_t: e.tensor_tensor(
                        out=x_t[:, j, n * 512:(n + 1) * 512], in0=x_t[:, j, n * 512:(n + 1) * 512], in1=p_t[:],
                        op=ALU.add), reads=[x_b, p_b], writes=[x_b])
            kb.dma("sp", xres[sq, s0:s0 + 512, :].rearrange("(j p) d -> p j d", p=128), x_t[:], reads=[x_b])


def p3b(kb, C, S, NSEQ, li, xres, gains, wg_ap, wu_ap, wd_ap):
    TT = 256
    NS = TT // 128
    with Phase(kb) as ph:
        wg = load_w(kb, ph, wg_ap, D, DFF, "wg")
        wu = load_w(kb, ph, wu_ap, D, DFF, "wu")
        wd = load_w(kb, ph, wd_ap, DFF, D, "wd")
        gain = ph.sb([128, 8], F32, "gfm")
        kb.dma("sp", gain[0][:], gains[1][DEPTH + li], writes=[gain[1]])
        nm = Norm(kb, ph, C, NS)
        x_r = ph.sb_ring(3, [128, NS, D], F32, "x")
        hn_r = ph.sb_ring(2, [128, NS, D], BF16, "hn")
        hT_r = ph.sb_ring(2, [128, 8, TT], BF16, "hT")
        aT_r = ph.sb_ring(1, [128, NF, TT], BF16, "aT")
        sg_r = ph.sb_ring(3, [128, TT], F32, "sg")
        pg_r = ph.ps_ring(4, [128, 512], F32, "pg")
        pd_r = ph.ps_ring(2, [128, 512], F32, "pd")
        NTILES = NSEQ * S // TT
        pre = {}

        def do_loads(T):
            sq, s0 = divmod(T * TT, S)
            x_t, x_b = x_r.next()
            kb.dma("sp", x_t[:], xres[sq, s0:s0 + TT, :].rearrange("(j p) d -> p j d", p=128), writes=[x_b])
            return x_t, x_b

        stA = {}

        def a_stats(T):
            x_t, x_b = pre.pop(T)
            stA[T] = (x_t, x_b, nm.stats(x_t, x_b))

        def a_norm(T):
            if T not in stA:
                a_stats(T)
            x_t, x_b, st = stA[T]
            hn_t, hn_b = hn_r.next()
            nm.apply(st, x_t, x_b, gain[0], gain[1], hn_t, hn_b)
            stA[T] = (x_t, x_b, hn_t, hn_b)

        def a_tr(T):
            x_t, x_b, hn_t, hn_b = stA[T]
            hT_t, hT_b = hT_r.next()
            nm.transpose(hn_t, hn_b, 8, hT_t, hT_b, gain=gain)
            stA[T] = (x_t, x_b, hT_t, hT_b)

        pre[0] = do_loads(0)
        if NTILES > 1:
            pre[1] = do_loads(1)
        a_norm(0)
        a_tr(0)
        for T in range(NTILES):
            sq, s0 = divmod(T * TT, S)
            x_t, x_b, hT_t, hT_b = stA.pop(T)
            if T + 2 < NTILES:
                pre[T + 2] = do_loads(T + 2)
            aT_t, aT_b = aT_r.next()
            for f in range(NF):
                if f == 2 and T + 1 < NTILES:
                    a_stats(T + 1)
                if f == 8 and T + 1 < NTILES:
                    a_norm(T + 1)
                if f == 14 and T + 1 < NTILES:
                    a_tr(T + 1)
                g_t, g_b = pg_r.next()
                u_t, u_b = pg_r.next()
                for (w, p_t, p_b) in ((wg, g_t, g_b), (wu, u_t, u_b)):
                    for c in range(8):
                        kb.op("pe", lambda e, c=c, f=f, w=w, p_t=p_t: e.matmul(
                            p_t[:, 0:TT], lhsT=w[0][:, c, f * 128:(f + 1) * 128], rhs=hT_t[:, c, :],
                            start=(c == 0), stop=(c == 7)), reads=[w[1], hT_b], writes=[p_b], signal=(c == 7))
                sg_t, sg_b = sg_r.next()
                kb.op("act", lambda e, g_t=g_t, sg_t=sg_t: e.activation(out=sg_t[:], in_=g_t[:, 0:TT], func=AF.Silu),
                      reads=[g_b], writes=[sg_b])
                kb.op("dve", lambda e, f=f, u_t=u_t, sg_t=sg_t: e.tensor_tensor(
                    out=aT_t[:, f, :], in0=sg_t[:], in1=u_t[:, 0:TT], op=ALU.mult),
                    reads=[sg_b, u_b], writes=[aT_b])
            for j in range(NS):
                for n in range(2):
                    p_t, p_b = pd_r.next()
                    for f in range(NF):
                        kb.op("pe", lambda e, f=f, n=n, j=j, p_t=p_t: e.matmul(
                            p_t[:], lhsT=aT_t[:, f, j * 128:(j + 1) * 128], rhs=wd[0][:, f, n * 512:(n + 1) * 512],
                            start=(f == 0), stop=(f == NF - 1)), reads=[aT_b, wd[1]], writes=[p_b],
                            signal=(f == NF - 1))
                    kb.op("dve", lambda e, n=n, j=j, p_t=p_t: e.tensor_tensor(
                        out=x_t[:, j, n * 512:(n + 1) * 512], in0=x_t[:, j, n * 512:(n + 1) * 512], in1=p_t[:],
                        op=ALU.add), reads=[x_b, p_b], writes=[x_b])
            kb.dma("sp", xres[sq, s0:s0 + TT, :].rearrange("(j p) d -> p j d", p=128), x_t[:], reads=[x_b])


def p3c(kb, C, S, NSEQ, li, xres, gains, wpg_ap, wpp_ap, p_ap, out_ap):
    with Phase(kb) as ph:
        wpg = load_w(kb, ph, wpg_ap, D, D, "wpg")
        wpp = load_w(kb, ph, wpp_ap, PLE, D, "wpp")
        gain = ph.sb([128, 8], F32, "gfm")
        kb.dma("sp", gain[0][:], gains[1][2 * DEPTH + li], writes=[gain[1]])
        fgain = load_bcast(kb, ph, gains[0][3 * DEPTH:3 * DEPTH + 1, :], D, "fgain") if out_ap is not None else None
        nm = Norm(kb, ph, C, 4)
        x_r = ph.sb_ring(3, [128, 4, D], F32, "x")
        p_r = ph.sb_ring(3, [128, 4, PLE], F32, "p")
        pb_r = ph.sb_ring(2, [128, 4, PLE], BF16, "pb")
        pT_r = ph.sb_ring(2, [128, 2, 512], BF16, "pT")
        hn_r = ph.sb_ring(2, [128, 4, D], BF16, "hn")
        hT_r = ph.sb_ring(2, [128, 8, 512], BF16, "hT")
        th_r = ph.sb_ring(3, [128, 512], F32, "th")
        tt_r = ph.sb_ring(3, [128, 512], F32, "tt")
        o_r = ph.sb_ring(2, [128, 4, D], F32, "o") if out_ap is not None else None
        ss_r = ph.sb_ring(2, [128, 8], F32, "ss2")
        pa_r = ph.ps_ring(3, [128, 512], F32, "pa")
        pq_r = ph.ps_ring(2, [128, 512], F32, "pq")
        NTILES = NSEQ * S // 512
        pre = {}

        def do_loads(T):
            sq, s0 = divmod(T * 512, S)
            x_t, x_b = x_r.next()
            p_t, p_b = p_r.next()
            kb.dma("sp", x_t[:], xres[sq, s0:s0 + 512, :].rearrange("(j p) d -> p j d", p=128), writes=[x_b])
            kb.dma("sp", p_t[:], p_ap[sq, s0:s0 + 512, :].rearrange("(j p) d -> p j d", p=128), writes=[p_b])
            return x_t, x_b, p_t, p_b

        stA = {}

        def a_stats(T):
            x_t, x_b, p_t, p_b = pre.pop(T)
            pb_t, pb_b = pb_r.next()
            kb.op("act", lambda e: e.copy(out=pb_t[:], in_=p_t[:]), reads=[p_b], writes=[pb_b])
            stA[T] = (x_t, x_b, pb_t, pb_b, nm.stats(x_t, x_b))

        def a_norm(T):
            if T not in stA:
                a_stats(T)
            x_t, x_b, pb_t, pb_b, st = stA[T]
            hn_t, hn_b = hn_r.next()
            nm.apply(st, x_t, x_b, gain[0], gain[1], hn_t, hn_b)
            stA[T] = (x_t, x_b, pb_t, pb_b, hn_t, hn_b)

        def a_tr(T):
            x_t, x_b, pb_t, pb_b, hn_t, hn_b = stA[T]
            pT_t, pT_b = pT_r.next()
            hT_t, hT_b = hT_r.next()
            nm.transpose(hn_t, hn_b, 8, hT_t, hT_b, gain=gain)
            nm.transpose(pb_t, pb_b, 2, pT_t, pT_b, alt=1)
            stA[T] = (x_t, x_b, pT_t, pT_b, hT_t, hT_b)

        pre[0] = do_loads(0)
        if NTILES > 1:
            pre[1] = do_loads(1)
        a_norm(0)
        a_tr(0)
        for T in range(NTILES):
            sq, s0 = divmod(T * 512, S)
            x_t, x_b, pT_t, pT_b, hT_t, hT_b = stA.pop(T)
            if T + 2 < NTILES:
                pre[T + 2] = do_loads(T + 2)
            for j in range(4):
                for n in range(2):
                    gi = j * 2 + n
                    if gi == 0 and T + 1 < NTILES:
                        a_stats(T + 1)
                    if gi == 3 and T + 1 < NTILES:
                        a_norm(T + 1)
                    if gi == 5 and T + 1 < NTILES:
                        a_tr(T + 1)
                    a_t, a_b = pa_r.next()
                    q_t, q_b = pq_r.next()
                    for c in range(8):
                        kb.op("pe", lambda e, c=c, n=n, j=j, a_t=a_t: e.matmul(
                            a_t[:], lhsT=hT_t[:, c, j * 128:(j + 1) * 128], rhs=wpg[0][:, c, n * 512:(n + 1) * 512],
                            start=(c == 0), stop=(c == 7)), reads=[hT_b, wpg[1]], writes=[a_b], signal=(c == 7))
                    for c in range(2):
                        kb.op("pe", lambda e, c=c, n=n, j=j, q_t=q_t: e.matmul(
                            q_t[:], lhsT=pT_t[:, c, j * 128:(j + 1) * 128], rhs=wpp[0][:, c, n * 512:(n + 1) * 512],
                            start=(c == 0), stop=(c == 1)), reads=[pT_b, wpp[1]], writes=[q_b], signal=(c == 1))
                    th_t, th_b = th_r.next()
                    tt_t, tt_b = tt_r.next()
                    kb.op("act", lambda e, a_t=a_t, th_t=th_t: e.activation(out=th_t[:], in_=a_t[:], func=AF.Tanh,
                                                                           scale=0.5), reads=[a_b], writes=[th_b])
                    kb.op("dve", lambda e, q_t=q_t, th_t=th_t, tt_t=tt_t: e.scalar_tensor_tensor(
                        out=tt_t[:], in0=th_t[:], scalar=1.0, in1=q_t[:], op0=ALU.add, op1=ALU.mult),
                        reads=[th_b, q_b], writes=[tt_b])
                    kb.op("dve", lambda e, n=n, j=j, tt_t=tt_t: e.scalar_tensor_tensor(
                        out=x_t[:, j, n * 512:(n + 1) * 512], in0=tt_t[:], scalar=0.5,
                        in1=x_t[:, j, n * 512:(n + 1) * 512], op0=ALU.mult, op1=ALU.add),
                        reads=[tt_b, x_b], writes=[x_b])
            if out_ap is None:
                kb.dma("sp", xres[sq, s0:s0 + 512, :].rearrange("(j p) d -> p j d", p=128), x_t[:], reads=[x_b])
            else:
                o_t, o_b = o_r.next()
                ss_t, ss_b = ss_r.next()
                for j in range(4):
                    kb.op("act", lambda e, j=j: e.activation(out=nm.junk[0][:], in_=x_t[:, j, :], func=AF.Square,
                                                             accum_out=ss_t[:, j:j + 1]),
                          reads=[x_b], writes=[nm.junk[1], ss_b])
                kb.op("dve", lambda e: e.tensor_scalar(out=ss_t[:, 0:4], in0=ss_t[:, 0:4], scalar1=1.0 / D, scalar2=EPS,
                                                       op0=ALU.mult, op1=ALU.add), reads=[ss_b], writes=[ss_b])
                kb.op("pool", lambda e: e.tensor_tensor(out=ss_t[:, 4:8], in0=ss_t[:, 0:4], in1=C["mhalf"][0][:, 0:4],
                                                        op=ALU.pow), reads=[ss_b, C["mhalf"][1]], writes=[ss_b])
                for j in range(4):
                    kb.op("dve", lambda e, j=j: e.scalar_tensor_tensor(
                        out=o_t[:, j, :], in0=x_t[:, j, :], scalar=ss_t[:, 4 + j:5 + j], in1=fgain[0][:],
                        op0=ALU.mult, op1=ALU.mult), reads=[x_b, ss_b, fgain[1]], writes=[o_b])
                kb.dma("sp", out_ap[sq, s0:s0 + 512, :].rearrange("(j p) d -> p j d", p=128), o_t[:], reads=[o_b])


def host_consts(S):
    half = 128
    freq = (1.0 / (10000.0 ** np.linspace(0.0, 1.0, half, dtype=np.float32))).astype(np.float32)
    freq_t = (freq.astype(np.float64) / (2.0 * np.pi)).astype(np.float32).reshape(128, 1)
    c = np.arange(256, dtype=np.float64)
    angc = 2.0 * np.pi * np.outer(c, c) / 256.0
    cd = (np.concatenate([np.cos(angc), -np.sin(angc)], axis=1) / np.sqrt(256.0 * S)).astype(np.float32)
    s = np.arange(S, dtype=np.int64)
    m = np.outer(s, s) % S
    ang = 2.0 * np.pi * m.astype(np.float64) / S
    dft = np.stack([np.cos(ang), np.sin(ang)]).astype(np.float32).astype(ml_dtypes.bfloat16)
    HT2 = (S // 128) // 2
    SG = min(8, HT2)
    NG = HT2 // SG
    alt = np.ascontiguousarray(dft[0, S // 2:S // 2 + 1, :])
    dft = dft[:, :S // 2, :]
    dft = dft.reshape(2, NG, SG, 128, S // 512, 512).transpose(4, 0, 1, 3, 2, 5).reshape(S // 512, 2, NG, 128, SG * 512)
    dft = np.ascontiguousarray(dft)
    return freq_t, cd, dft, alt


def make_in_maps(inputs, S, NSEQ, ncores):
    f = lambda a: np.ascontiguousarray(np.asarray(a))
    freq_t, cd, dft, alt = host_consts(S)
    gains = f(np.concatenate([inputs["norm_mix"], inputs["norm_ffn"], inputs["norm_ple"],
                              np.asarray(inputs["final_norm"]).reshape(1, D)], axis=0).astype(np.float32))
    shared = {
        "gains": gains,
        "gfm": f(gains.reshape(3 * DEPTH + 1, 8, 128).transpose(0, 2, 1)),
        "ret_w_in": f(inputs["ret_w_in"]), "ret_w_out": f(inputs["ret_w_out"]), "ret_gn": f(inputs["ret_gn_gain"]),
        "ret_dl": f(np.asarray(inputs["ret_decay_logit"]).reshape(1, 16)),
        "fno_w": f(inputs["fno_w_out"]), "w_gate": f(inputs["ffn_w_gate"]), "w_up": f(inputs["ffn_w_up"]),
        "w_down": f(inputs["ffn_w_down"]), "w_pg": f(inputs["ple_w_gate"]), "w_pp": f(inputs["ple_w_proj"]),
        "freq": freq_t, "cd": cd, "dft": dft, "alt": alt,
    }
    x = np.asarray(inputs["x"])
    p = np.asarray(inputs["p"])
    pos = np.asarray(inputs["positions"]).astype(np.int32)
    maps = []
    for c in range(ncores):
        m = dict(shared)
        m["x"] = f(x[c * NSEQ:(c + 1) * NSEQ])
        m["p"] = f(p[:, c * NSEQ:(c + 1) * NSEQ])
        m["pos"] = f(pos[c * NSEQ:(c + 1) * NSEQ])
        maps.append(m)
    return maps


_NC_CACHE = {}


def kernel(**inputs):
    x = np.asarray(inputs["x"])
    B, S, _ = x.shape
    ncores = 8
    NSEQ = B // ncores
    key = (S, NSEQ)
    if key not in _NC_CACHE:
        _NC_CACHE[key] = build(S, NSEQ, [0, 1, 2, 3])
    nc = _NC_CACHE[key]
    maps = make_in_maps(inputs, S, NSEQ, ncores)
    res = run_bass_kernel_spmd(nc, maps, core_ids=list(range(ncores)))
    return np.concatenate([r["out"] for r in res.results], axis=0).astype(np.float32)
```
